# Optimizing a Trainium2 kernel written in Bass

```python
import math
import jax, jax.numpy as jnp
from jax import lax
import numpy as np

D_MODEL = 1024
BATCH = 16
SEQ = 2048
DEPTH = 1

D_MIX = D_MODEL
GM_WIDTH = D_MIX // 2
GM_HEAD_DIM = 64
GM_HEADS = GM_WIDTH // GM_HEAD_DIM
GM_CHUNK = 128
SSM_WIDTH = D_MIX - GM_WIDTH
SSM_HEAD_DIM = 64
SSM_HEADS = SSM_WIDTH // SSM_HEAD_DIM
SSM_GROUPS = 2
SSM_STATE = 128
SSM_CONV = 4
SSM_CHUNK = 128
SSM_CONV_CH = SSM_WIDTH + 2 * SSM_GROUPS * SSM_STATE
D_FF = 4 * D_MODEL
EPS = 1e-6

IN_COLS = 2 * GM_WIDTH + SSM_WIDTH + SSM_CONV_CH + SSM_HEADS
SPLITS = (GM_WIDTH, 2 * GM_WIDTH, 2 * GM_WIDTH + SSM_WIDTH,
          2 * GM_WIDTH + SSM_WIDTH + SSM_CONV_CH)

kernel_name = "hybrid_gmlp_ssd_sandwich_block"


def rms_norm(x, w):
    xf = x.astype(jnp.float32)
    y = xf * lax.rsqrt(jnp.mean(xf * xf, axis=-1, keepdims=True) + EPS)
    return (y * w.astype(jnp.float32)).astype(x.dtype)


def layer_norm(x, w, b):
    xf = x.astype(jnp.float32)
    mu = jnp.mean(xf, axis=-1, keepdims=True)
    var = jnp.mean(jnp.square(xf - mu), axis=-1, keepdims=True)
    y = (xf - mu) * lax.rsqrt(var + EPS)
    return (y * w.astype(jnp.float32) + b.astype(jnp.float32)).astype(x.dtype)


def gmlp_mixer(u, v, ln_w, ln_b, w_s, b_s):
    bsz, L, _ = u.shape
    nc = L // GM_CHUNK
    u = jax.nn.gelu(u)
    v = jax.nn.gelu(v).reshape(bsz, L, GM_HEADS, GM_HEAD_DIM)
    v = layer_norm(v, ln_w, ln_b).reshape(bsz, nc, GM_CHUNK, GM_HEADS, GM_HEAD_DIM)
    causal = jnp.tril(jnp.ones((GM_CHUNK, GM_CHUNK), dtype=bool))
    w = jnp.where(causal[None], w_s, jnp.zeros((), w_s.dtype))
    mixed = jnp.einsum("hts,bcshp->bcthp", w, v) + b_s.T[None, None, :, :, None]
    return u * mixed.reshape(bsz, L, GM_WIDTH)


def causal_depthwise_conv(x, w, b):
    ch = x.shape[-1]
    y = lax.conv_general_dilated(
        x, w[:, None, :].astype(x.dtype), window_strides=(1,),
        padding=((SSM_CONV - 1, 0),), dimension_numbers=("NWC", "WIO", "NWC"),
        feature_group_count=ch)
    return y + b


def ssd_chunked(x, dt, a, bmat, cmat, d_skip):
    bsz, L = x.shape[0], x.shape[1]
    nc = L // SSM_CHUNK
    R = SSM_HEADS // SSM_GROUPS
    Q = SSM_CHUNK
    x = x.astype(jnp.float32).reshape(bsz, nc, Q, SSM_GROUPS, R, SSM_HEAD_DIM)
    dt = dt.astype(jnp.float32).reshape(bsz, nc, Q, SSM_GROUPS, R)
    bmat = bmat.astype(jnp.float32).reshape(bsz, nc, Q, SSM_GROUPS, SSM_STATE)
    cmat = cmat.astype(jnp.float32).reshape(bsz, nc, Q, SSM_GROUPS, SSM_STATE)
    a = a.astype(jnp.float32).reshape(SSM_GROUPS, R)
    d_skip = d_skip.astype(jnp.float32).reshape(SSM_GROUPS, R)

    a_cs = jnp.cumsum(dt * a, axis=2)
    x_dt = x * dt[..., None]

    causal = jnp.tril(jnp.ones((Q, Q), dtype=bool))[:, :, None, None]
    seg = a_cs[:, :, :, None] - a_cs[:, :, None, :]
    decay = jnp.exp(jnp.where(causal, seg, -jnp.inf))
    cb = jnp.einsum("bclgn,bcsgn->bclsg", cmat, bmat)
    y_diag = jnp.einsum("bclsg,bclsgr,bcsgrp->bclgrp", cb, decay, x_dt)

    decay_to_end = jnp.exp(a_cs[:, :, -1:] - a_cs)
    states = jnp.einsum("bcsgn,bcsgr,bcsgrp->bcgrpn", bmat, decay_to_end, x_dt)
    chunk_decay = jnp.exp(a_cs[:, :, -1])

    def step(h, inp):
        s, dcy = inp
        return h * dcy[..., None, None] + s, h
    h0 = jnp.zeros((bsz, SSM_GROUPS, R, SSM_HEAD_DIM, SSM_STATE), jnp.float32)
    _, prev = lax.scan(step, h0, (jnp.moveaxis(states, 1, 0), jnp.moveaxis(chunk_decay, 1, 0)))
    prev = jnp.moveaxis(prev, 0, 1)

    y_off = jnp.einsum("bclgn,bcgrpn,bclgr->bclgrp", cmat, prev, jnp.exp(a_cs))
    y = y_diag + y_off + d_skip[:, :, None] * x
    return y.reshape(bsz, L, SSM_HEADS * SSM_HEAD_DIM)


def mamba2_mixer(z, xbc, dt_raw, conv_w, conv_b, dt_bias, a_log, d_skip, norm_w):
    bsz, L, _ = z.shape
    xbc = jax.nn.silu(causal_depthwise_conv(xbc, conv_w, conv_b))
    xs = xbc[..., :SSM_WIDTH].reshape(bsz, L, SSM_HEADS, SSM_HEAD_DIM)
    bmat = xbc[..., SSM_WIDTH:SSM_WIDTH + SSM_GROUPS * SSM_STATE].reshape(bsz, L, SSM_GROUPS, SSM_STATE)
    cmat = xbc[..., SSM_WIDTH + SSM_GROUPS * SSM_STATE:].reshape(bsz, L, SSM_GROUPS, SSM_STATE)
    dt = jax.nn.softplus(dt_raw.astype(jnp.float32) + dt_bias.astype(jnp.float32))
    a = -jnp.exp(a_log.astype(jnp.float32))
    y = ssd_chunked(xs, dt, a, bmat, cmat, d_skip)
    y = y * jax.nn.silu(z.astype(jnp.float32))
    y = y.reshape(bsz, L, SSM_GROUPS, SSM_WIDTH // SSM_GROUPS)
    y = y * lax.rsqrt(jnp.mean(y * y, axis=-1, keepdims=True) + EPS)
    y = y.reshape(bsz, L, SSM_WIDTH) * norm_w.astype(jnp.float32)
    return y.astype(z.dtype)


def setup_inputs(seed: int = 0) -> dict:
    key = jax.random.key(seed)
    ks = jax.random.split(key, 24)
    f32 = jnp.float32

    def gain(k, shape):
        return 1.0 + 0.02 * jax.random.normal(k, shape, f32)

    x = jax.random.normal(ks[0], (BATCH, SEQ, D_MODEL), f32)
    norm_mix_pre = gain(ks[1], (DEPTH, D_MODEL))
    w_in = jax.random.normal(ks[2], (DEPTH, D_MODEL, IN_COLS), f32) * D_MODEL ** -0.5
    gm_ln_w = gain(ks[3], (DEPTH, GM_HEADS, GM_HEAD_DIM))
    gm_ln_b = 0.02 * jax.random.normal(ks[4], (DEPTH, GM_HEADS, GM_HEAD_DIM), f32)
    gm_w_s = jax.random.normal(ks[5], (DEPTH, GM_HEADS, GM_CHUNK, GM_CHUNK), f32) * GM_CHUNK ** -0.5
    gm_b_s = gain(ks[6], (DEPTH, GM_HEADS, GM_CHUNK))
    conv_w = jax.random.normal(ks[7], (DEPTH, SSM_CONV, SSM_CONV_CH), f32) * SSM_CONV ** -0.5
    conv_b = 0.02 * jax.random.normal(ks[8], (DEPTH, SSM_CONV_CH), f32)
    dt_min, dt_max = 1e-3, 1e-1
    u = jax.random.uniform(ks[9], (DEPTH, SSM_HEADS), f32)
    dt0 = jnp.maximum(jnp.exp(u * (math.log(dt_max) - math.log(dt_min)) + math.log(dt_min)), 1e-4)
    dt_bias = dt0 + jnp.log(-jnp.expm1(-dt0))
    a_log = jnp.log(jax.random.uniform(ks[10], (DEPTH, SSM_HEADS), f32, 1.0, 16.0))
    d_skip = gain(ks[11], (DEPTH, SSM_HEADS))
    ssm_norm_w = gain(ks[12], (DEPTH, SSM_WIDTH))
    w_out = jax.random.normal(ks[13], (DEPTH, D_MIX, D_MODEL), f32) * D_MIX ** -0.5
    norm_mix_post = gain(ks[14], (DEPTH, D_MODEL))
    norm_ffn_pre = gain(ks[15], (DEPTH, D_MODEL))
    w_up = jax.random.normal(ks[16], (DEPTH, D_MODEL, D_FF), f32) * D_MODEL ** -0.5
    w_down = jax.random.normal(ks[17], (DEPTH, D_FF, D_MODEL), f32) * D_FF ** -0.5
    norm_ffn_post = gain(ks[18], (DEPTH, D_MODEL))
    return {"x": x, "norm_mix_pre": norm_mix_pre, "w_in": w_in, "gm_ln_w": gm_ln_w,
            "gm_ln_b": gm_ln_b, "gm_w_s": gm_w_s, "gm_b_s": gm_b_s, "conv_w": conv_w,
            "conv_b": conv_b, "dt_bias": dt_bias, "a_log": a_log, "d_skip": d_skip,
            "ssm_norm_w": ssm_norm_w, "w_out": w_out, "norm_mix_post": norm_mix_post,
            "norm_ffn_pre": norm_ffn_pre, "w_up": w_up, "w_down": w_down,
            "norm_ffn_post": norm_ffn_post}


def reference(x, norm_mix_pre, w_in, gm_ln_w, gm_ln_b, gm_w_s, gm_b_s, conv_w, conv_b,
              dt_bias, a_log, d_skip, ssm_norm_w, w_out, norm_mix_post, norm_ffn_pre,
              w_up, w_down, norm_ffn_post):
    for i in range(DEPTH):
        h = rms_norm(x, norm_mix_pre[i])
        proj = jnp.einsum("bld,dk->blk", h, w_in[i])
        u_a, v_a, z_b, xbc_b, dt_b = jnp.split(proj, SPLITS, axis=-1)
        y_a = gmlp_mixer(u_a, v_a, gm_ln_w[i], gm_ln_b[i], gm_w_s[i], gm_b_s[i])
        y_b = mamba2_mixer(z_b, xbc_b, dt_b, conv_w[i], conv_b[i], dt_bias[i], a_log[i],
                           d_skip[i], ssm_norm_w[i])
        mix = jnp.concatenate([y_a, y_b], axis=-1)
        x = x + rms_norm(jnp.einsum("blk,kd->bld", mix, w_out[i]), norm_mix_post[i])
        h = rms_norm(x, norm_ffn_pre[i])
        f = jnp.square(jax.nn.relu(jnp.einsum("bld,df->blf", h, w_up[i])))
        x = x + rms_norm(jnp.einsum("blf,fd->bld", f, w_down[i]), norm_ffn_post[i])
    return x
```

```python
from contextlib import ExitStack
import numpy as np
import os
XV = os.environ.get('XV', '')
import concourse.bass as bass
import concourse.mybir as mybir
from concourse.bass_utils import run_bass_kernel_spmd

F32 = mybir.dt.float32
BF16 = mybir.dt.bfloat16
AF = mybir.ActivationFunctionType
ALU = mybir.AluOpType
AX = mybir.AxisListType

EPS = 1e-6
NSLOT = 23
RING = 4
DOWN_BASE = 15
R_POST, R_POST2, R_LNW, R_LNB, R_SSMW, R_DTB, R_ALOG, R_DSK, R_END = 0, 1024, 2048, 2560, 3072, 3584, 3592, 3600, 3608
C_WPRE, C_WFFN, C_CW, C_CB, C_END = 0, 8, 16, 48, 56


class Tile:
    def __init__(self, t, name):
        self.t = t
        self.name = name
        self.w = None
        self.r = {}
        self.const = False
        self.psum = False


class Eng:
    def __init__(self, name, sem):
        self.name = name
        self.sem = sem
        self.cnt = 0
        self.seen = {}
        self.prog = []
        self.dsems = []
        self.dval = []
        self.dk = 0


class Planner:
    def __init__(self, nc, es):
        self.nc = nc
        self.es = es
        self.sems = {}
        self.E = {}
        for n in ("pe", "act", "dve", "pool", "sp"):
            s = es.enter_context(nc.semaphore("sem_" + n))
            self.sems[n] = s
            self.E[n] = Eng(n, s)
        for q, k in (("sp", 12), ("pool", 26)):
            for j in range(k):
                key = "d_%s_%d" % (q, j)
                self.sems[key] = es.enter_context(nc.semaphore(key))
                self.E[q].dsems.append(key)
                self.E[q].dval.append(0)
        self.nsb = 0

    def sb(self, name, cols, dt):
        t = self.es.enter_context(self.nc.sbuf_tensor("s_" + name, [128, cols], dt))
        return Tile(t, name)

    def sb_multi(self, name, cols, dt, nparts):
        t = self.es.enter_context(self.nc.sbuf_tensor("s_" + name, [128, cols], dt))
        return t, [Tile(t, "%s_%d" % (name, i)) for i in range(nparts)]

    def ps(self, name, cols, dt):
        t = self.es.enter_context(self.nc.psum_tensor("p_" + name, [128, cols], dt))
        tl = Tile(t, name)
        tl.psum = True
        return tl

    def _deps(self, E, reads, writes):
        deps = {}

        def add(tk):
            if tk is None:
                return
            k, v = tk
            if deps.get(k, 0) < v:
                deps[k] = v

        for t in reads:
            add(t.w)
            if t.psum:
                for k, v in t.r.items():
                    if k != E.name:
                        add((k, v))
        for t in writes:
            add(t.w)
            for k, v in t.r.items():
                add((k, v))
        for k, v in deps.items():
            if k == E.name and E.name == "pe":
                continue
            if E.seen.get(k, 0) >= v:
                continue
            E.seen[k] = v
            sem = self.sems[k]
            E.prog.append(lambda e, sem=sem, v=v: e.wait_ge(sem, v))

    def _record(self, tk, reads, writes):
        k, v = tk
        for t in reads:
            if not t.const:
                if t.r.get(k, 0) < v:
                    t.r[k] = v
        for t in writes:
            t.w = tk
            t.r = {}

    def op(self, eng, fn, reads=(), writes=()):
        E = self.E[eng]
        self._deps(E, reads, writes)
        E.cnt += 1
        sem = E.sem
        E.prog.append(lambda e, fn=fn, sem=sem: fn(e).then_inc(sem, 1))
        self._record((E.name, E.cnt), reads, writes)

    def pe(self, fns, reads=(), writes=(), inc=True):
        E = self.E["pe"]
        self._deps(E, reads, writes)
        if not inc:
            for fn in fns:
                E.prog.append(lambda e, fn=fn: fn(e))
            self._record((E.name, E.cnt + 1), reads, writes)
            return
        for fn in fns[:-1]:
            E.prog.append(lambda e, fn=fn: fn(e))
        E.cnt += 1
        sem = E.sem
        E.prog.append(lambda e, fn=fns[-1], sem=sem: fn(e).then_inc(sem, 1))
        self._record((E.name, E.cnt), reads, writes)

    def dma(self, q, out, in_, reads=(), writes=(), accum=False):
        E = self.E[q]
        self._deps(E, reads, writes)
        k = E.dk
        E.dk = (E.dk + 1) % len(E.dsems)
        key = E.dsems[k]
        sem = self.sems[key]
        prev = E.dval[k]
        if prev > 0 and E.seen.get(key, 0) < prev:
            E.seen[key] = prev
            E.prog.append(lambda e, sem=sem, v=prev: e.wait_ge(sem, v))
        E.dval[k] += 16
        if accum:
            E.prog.append(lambda e, out=out, in_=in_, sem=sem: e.dma_start(out=out, in_=in_, accum_op=ALU.add).then_inc(sem, 16))
        else:
            E.prog.append(lambda e, out=out, in_=in_, sem=sem: e.dma_start(out=out, in_=in_).then_inc(sem, 16))
        self._record((key, E.dval[k]), reads, writes)

    def finish(self):
        E = self.E["sp"]
        for q in ("sp", "pool"):
            Q = self.E[q]
            for key, v in zip(Q.dsems, Q.dval):
                if v > 0:
                    sem = self.sems[key]
                    E.prog.append(lambda e, sem=sem, v=v: e.wait_ge(sem, v))
        for n in ("pe", "act", "dve", "pool"):
            v = self.E[n].cnt
            if v > 0:
                sem = self.sems[n]
                E.prog.append(lambda e, sem=sem, v=v: e.wait_ge(sem, v))

    def emit(self):
        nc = self.nc
        E = self.E
        with nc.Block() as block:
            @block.sync
            def _(e):
                for f in E["sp"].prog:
                    f(e)

            @block.tensor
            def _(e):
                for f in E["pe"].prog:
                    f(e)

            @block.scalar
            def _(e):
                for f in E["act"].prog:
                    f(e)

            @block.vector
            def _(e):
                for f in E["dve"].prog:
                    f(e)

            @block.gpsimd
            def _(e):
                for f in E["pool"].prog:
                    f(e)


def v3(ap, a):
    return ap.rearrange("p (a b) -> p a b", a=a)


class _Stop(Exception):
    pass


def build(NG=8, stop=0):
    nc = bass.Bass("TRN2", target_bir_lowering=False)
    es = ExitStack()
    NT = NG * 512
    x_d = nc.dram_tensor("x", [NT, 1024], F32, kind="ExternalInput").ap()
    out_d = nc.dram_tensor("out", [NT, 1024], F32, kind="ExternalOutput").ap()
    wsl_d = nc.dram_tensor("wslots", [NSLOT, 128, 4096], F32, kind="ExternalInput").ap()
    wdt_d = nc.dram_tensor("wdt", [128, 64], F32, kind="ExternalInput").ap()
    rowp_d = nc.dram_tensor("rowpack", [R_END], F32, kind="ExternalInput").ap()
    colp_d = nc.dram_tensor("colpack", [128, C_END], F32, kind="ExternalInput").ap()
    wst_d = nc.dram_tensor("wst", [128, 1024], F32, kind="ExternalInput").ap()
    bs_d = nc.dram_tensor("bsrow", [1, 1024], F32, kind="ExternalInput").ap()
    scr_d = nc.dram_tensor("scr", [NSLOT, 128, 4096], BF16, kind="Internal").ap()

    P = Planner(nc, es)
    op, pe, dma = P.op, P.pe, P.dma

    ssD = [P.sb("ssD%d" % i, 4, F32) for i in range(4)]
    rsD = [P.sb("rsD%d" % i, 2, F32) for i in range(4)]
    fT_t, fT = P.sb_multi("fT", 32 * 512, BF16, 32)
    rowb = P.sb("rowb", R_END, F32)
    colp = P.sb("colp", C_END, F32)
    ident = P.sb("ident", 128, BF16)
    onesf = P.sb("onesf", 128, F32)
    trif = P.sb("trif", 128, F32)
    trib = P.sb("trib", 128, BF16)
    m1b = P.sb("m1b", 128, BF16)
    wsT = P.sb("wsT", 1024, BF16)
    dsk = P.sb("dsk", 1024, BF16)
    bsf = es.enter_context(nc.sbuf_tensor("s_bsf", [1, 1024], F32))
    bsb_t = es.enter_context(nc.sbuf_tensor("s_bsb", [1, 1024], BF16))
    bsb = Tile(bsb_t, "bsb")
    bsfT = Tile(bsf, "bsf")
    onesr_t = es.enter_context(nc.sbuf_tensor("s_onesr", [1, 64], BF16))
    onesr = Tile(onesr_t, "onesr")
    nh = P.sb("nh", 8, F32)
    ab = P.sb("ab", 8, F32)
    wdt = P.sb("wdt", 64, BF16)

    ring = [P.sb("ring%d" % r, 4096, BF16) for r in range(RING)]
    xs = [P.sb("xs%d" % i, 1024, F32) for i in range(4)]
    hb = [P.sb("hb%d" % i, 1024, BF16) for i in range(2)]
    junk = P.sb("junk", 1024, BF16)
    hT_t, hT = P.sb_multi("hT", 4096, BF16, 4)
    h2T_t, h2T = P.sb_multi("h2T", 4096, BF16, 4)
    vg = [P.sb("vg0", 512, F32)] * 2
    sq = P.sb("sq", 512, F32)
    vn = P.sb("vn", 512, F32)
    vl = [P.sb("vl%d" % i, 512, BF16) for i in range(4)]
    xraw_t, xraw = P.sb_multi("xraw", 8 * 516, BF16, 8)
    cdg = P.sb("cdg", 32 * 128, BF16)
    xact_t, xact = P.sb_multi("xact", 4096, BF16, 8)
    xB_t, xB = P.sb_multi("xB", 4 * 768, BF16, 4)
    mixt_t, mixt = P.sb_multi("mixt", 4096, BF16, 4)
    Rb2 = [P.sb("Rb%d" % i, 1024, BF16) for i in range(2)]
    dec2 = [P.sb("dec%d" % i, 1024, BF16) for i in range(1)] * 2
    cbm = P.sb("cbm", 256, BF16)
    t1 = P.sb("t1", 512, F32)
    xw = P.sb("xw", 512, BF16)
    state = P.sb("state", 512, F32)
    stbf = P.sb("stbf", 512, BF16)
    tt_ = [P.sb("tt%d" % i, 512, F32) for i in range(2)]
    rl = [P.sb("rl%d" % i, 512, F32) for i in range(2)]
    ttF = [P.sb("ttF%d" % i, 512, F32) for i in range(2)]
    o2k = P.sb("o2k", 1024, BF16)

    def smalls(name, n, cnt):
        return [P.sb("%s%d" % (name, i), n, F32) for i in range(cnt)]

    ssA = smalls("ssA", 2, 4)
    rsA = smalls("rsA", 2, 4)
    lnS = smalls("lnS", 48, 2)
    dtr = P.sb("dtr", 32, F32)
    dta = P.sb("dta", 32, F32)
    dte = P.sb("dte", 32, F32)
    dtv = P.sb("dtv", 32, F32)
    dav = P.sb("dav", 32, F32)
    ein4 = P.sb("ein4", 128, F32)
    eout4 = P.sb("eout4", 128, F32)
    ssq = smalls("ssq", 4, 2)
    rg = smalls("rg", 2, 2)
    ssO = smalls("ssO", 4, 4)
    rsO = smalls("rsO", 2, 4)
    ssF = smalls("ssF", 2, 4)
    rsF = smalls("rsF", 2, 4)

    pb = [P.ps("pb%d" % i, 512, F32) for i in range(6)]
    ptb = [P.ps("ptb%d" % i, 1024, BF16) for i in range(2)]

    scrT = [Tile(None, "scr%d" % s) for s in range(NSLOT)]

    def A(tile):
        return tile.t[:]

    def act(out, in_, func, reads, writes, bias=None, scale=None, accum=None):
        kw = {}
        if bias is not None:
            kw["bias"] = bias
        if scale is not None:
            kw["scale"] = scale
        if accum is not None:
            kw["accum_out"] = accum
        op("act", lambda e: e.activation(out=out, in_=in_, func=func, **kw), reads, writes)

    def ts(eng, out, in0, s1, op0, reads, writes, s2=None, op1=None):
        if op1 is None and eng == "pool":
            op(eng, lambda e: e.tensor_scalar(out=out, in0=in0, scalar1=s1, scalar2=1.0, op0=op0, op1=ALU.mult), reads, writes)
        elif op1 is None:
            op(eng, lambda e: e.tensor_scalar(out=out, in0=in0, scalar1=s1, scalar2=None, op0=op0), reads, writes)
        else:
            op(eng, lambda e: e.tensor_scalar(out=out, in0=in0, scalar1=s1, scalar2=s2, op0=op0, op1=op1), reads, writes)

    def tt(eng, out, in0, in1, o, reads, writes):
        op(eng, lambda e: e.tensor_tensor(out=out, in0=in0, in1=in1, op=o), reads, writes)

    def stt(out, in0, scalar, in1, op0, op1, reads, writes):
        op("dve", lambda e: e.scalar_tensor_tensor(out=out, in0=in0, scalar=scalar, in1=in1, op0=op0, op1=op1), reads, writes)

    def cp(eng, out, in_, reads, writes):
        if eng == "act":
            op(eng, lambda e: e.activation(out=out, in_=in_, func=AF.Copy), reads, writes)
        else:
            op(eng, lambda e: e.tensor_copy(out=out, in_=in_), reads, writes)

    def mm(out, lhsT, rhs, start, stop):
        return lambda e: e.matmul(out=out, lhsT=lhsT, rhs=rhs, start=start, stop=stop)

    def tr(out, in_):
        idn = ident.t[:]
        return lambda e: e.transpose(out=out, in_=in_, identity=idn)

    def rsqrt_small(dst, src, tmp, scale, n, reads_t, tmp_t, dst_t):
        ts("pool", tmp, src, scale, ALU.mult, [reads_t], [tmp_t], s2=EPS, op1=ALU.add)
        tt("pool", dst, tmp, nh.t[:, 0:n], ALU.pow, [tmp_t, nh], [dst_t])

    ring_state = {"m": 0, "f": 0}

    def load_slot(s, pool="m"):
        k = ring_state[pool]
        ring_state[pool] = k + 1
        r = (k % 2) + (0 if pool == "m" else 2)
        assert scrT[s].w is not None, "weight slot %d used before its conversion was issued" % s
        dma("sp", ring[r].t[:], scr_d[s], reads=[scrT[s]], writes=[ring[r]])
        return ring[r]

    def ck(n, src_ap=None, reads=()):
        if stop != n:
            return
        if src_ap is not None:
            ncol = src_ap.shape[-1]
            cp("dve", vn.t[:, 0:ncol], src_ap, list(reads), [vn])
            dma("sp", out_d[0:128, 0:ncol], vn.t[:, 0:ncol], [vn], [])
        raise _Stop()

    dma("sp", rowb.t[:], rowp_d.partition_broadcast(128), [], [rowb])
    dma("sp", colp.t[:], colp_d, [], [colp])
    dma("sp", xs[0].t[:], wst_d, [], [xs[0]])
    dma("sp", bsf[:], bs_d, [], [bsfT])
    dma("pool", wdt.t[:], wdt_d, [], [wdt])
    op("pool", lambda e: e.memset(onesf.t[:], 1.0), [], [onesf])
    op("pool", lambda e: e.memset(nh.t[:], -0.5), [], [nh])
    op("pool", lambda e: e.memset(onesr_t[:], 1.0), [], [onesr])
    op("pool", lambda e: e.affine_select(out=ident.t[:], in_=onesf.t[:], pattern=[[-1, 128]], compare_op=ALU.is_equal,
                                          fill=0.0, base=0, channel_multiplier=1), [onesf], [ident])
    op("pool", lambda e: e.affine_select(out=trif.t[:], in_=onesf.t[:], pattern=[[1, 128]], compare_op=ALU.is_ge,
                                          fill=0.0, base=0, channel_multiplier=-1), [onesf], [trif])
    op("pool", lambda e: e.affine_select(out=trib.t[:], in_=onesf.t[:], pattern=[[1, 128]], compare_op=ALU.is_ge,
                                          fill=0.0, base=0, channel_multiplier=-1), [onesf], [trib])
    op("pool", lambda e: e.affine_select(out=m1b.t[:], in_=onesf.t[:], pattern=[[-1, 128]], compare_op=ALU.is_gt,
                                          fill=0.0, base=0, channel_multiplier=1), [onesf], [m1b])
    conv_order = [1, 2, 0, 3, 4, 5, 6] + list(range(7, NSLOT))
    conv_state = {"k": 0}

    def convert_some(n):
        for _ in range(n):
            k = conv_state["k"]
            if k >= NSLOT:
                return
            sl_ = conv_order[k]
            conv_state["k"] = k + 1
            dma("pool", scr_d[sl_], wsl_d[sl_], [], [scrT[sl_]])

    convert_some(5)
    tt("dve", v3(wsT.t[:], 8), v3(xs[0].t[:], 8), trif.t[:].unsqueeze(1).to_broadcast([128, 8, 128]), ALU.mult,
       [xs[0], trif], [wsT])
    cp("dve", bsb_t[:], bsf[:], [bsfT], [bsb])
    act(ab.t[:], rowb.t[:, R_ALOG:R_ALOG + 8], AF.Exp, [rowb], [ab])
    ts("dve", ab.t[:], ab.t[:], -1.0, ALU.mult, [ab], [ab])
    for h in range(8):
        ts("dve", dsk.t[:, h * 128:(h + 1) * 128], ident.t[:], rowb.t[:, R_DSK + h:R_DSK + h + 1], ALU.mult,
           [ident, rowb], [dsk])
    for j in range(32):
        ts("dve", cdg.t[:, j * 128:(j + 1) * 128], ident.t[:], colp.t[:, C_CW + j:C_CW + j + 1], ALU.mult, [ident, colp], [cdg])
    for t_ in (rowb, colp, ident, onesf, trif, trib, m1b, wsT, dsk, bsb, onesr, nh, ab, wdt, cdg):
        t_.const = True
    for t_ in scrT:
        t_.const = True

    try:
        ck(1, wsT.t[:, 0:512], [wsT])
    except _Stop:
        P.finish(); P.emit(); return nc
    hT3 = v3(hT_t[:], 8)
    h2T3 = v3(h2T_t[:], 8)
    xraw3 = v3(xraw_t[:], 8)
    cdg3 = v3(cdg.t[:], 32)
    xact3 = v3(xact_t[:], 8)
    xB3 = v3(xB_t[:], 4)
    mixt3 = v3(mixt_t[:], 4)
    fT3 = v3(fT_t[:], 32)
    wsT3 = v3(wsT.t[:], 8)
    dsk3 = v3(dsk.t[:], 8)
    wdt3 = v3(wdt.t[:], 8)

    def wpre_b(off):
        return colp.t[:, off:off + 8].unsqueeze(2).to_broadcast([128, 8, 128])

    bank_rr = {"k": 0}

    def mmbank(n):
        b = bank_rr["k"] % n
        bank_rr["k"] += 1
        return pb[b]

    tb_rr = {"k": 0}

    def tbank():
        b = tb_rr["k"] % 2
        tb_rr["k"] += 1
        return ptb[b]

    M = pb[0:4]
    Fb = pb[4:6]

    pre_issued = {}
    preload = {}

    def mixer(g):
        tok0 = g * 512
        seq_start = (g % 4 == 0)
        outT = [Tile(None, "out_%d_%d" % (g, i)) for i in range(4)]

        def prodA1(i, base=None):
            r0 = (tok0 if base is None else base) + i * 128
            dma("sp", xs[i].t[:], x_d[r0:r0 + 128, :], [], [xs[i]])
            act(junk.t[:], xs[i].t[:], AF.Square, [xs[i]], [junk, ssA[i]], accum=ssA[i].t[:, 0:1])
            rsqrt_small(rsA[i].t[:, 0:1], ssA[i].t[:, 0:1], ssA[i].t[:, 1:2], 1.0 / 1024, 1, ssA[i], ssA[i], rsA[i])

        def prodA(i, base=None, front=True):
            if front:
                prodA1(i, base)
            ts("dve", hb[i % 2].t[:], xs[i].t[:], rsA[i].t[:, 0:1], ALU.mult, [xs[i], rsA[i]], [hb[i % 2]])

        def consA(i):
            h_ = hb[i % 2]
            tb = tbank()
            pe([tr(tb.t[:, k * 128:(k + 1) * 128], h_.t[:, k * 128:(k + 1) * 128]) for k in range(8)], [h_, ident], [tb])
            tt("dve", hT3[:, :, i * 128:(i + 1) * 128], v3(tb.t[:], 8), wpre_b(C_WPRE), ALU.mult, [tb, colp], [hT[i]])

        if not pre_issued.get(g):
            prodA(0)
            prodA(1)
            yield
            yield
        for i in range(4):
            consA(i)
            if i + 2 < 4:
                prodA(i + 2, front=not pre_issued.get(g))
            yield
        ck(2, hT3[:, 0, :], hT)

        pdt = mmbank(4)
        for i in range(4):
            pe([mm(pdt.t[:, i * 8:(i + 1) * 8], hT3[:, k, i * 128:(i + 1) * 128], wdt3[:, k, :], k == 0, k == 7) for k in range(8)],
               [hT[i], wdt], [pdt])
        tt("dve", v3(dtr.t[:], 4), v3(pdt.t[:, 0:32], 4), rowb.t[:, R_DTB:R_DTB + 8].unsqueeze(1).to_broadcast([128, 4, 8]),
           ALU.add, [pdt, rowb], [dtr])
        stt(dta.t[:], dtr.t[:], -1.0, dtr.t[:], ALU.mult, ALU.max, [dtr], [dta])
        act(dte.t[:], dta.t[:], AF.Exp, [dta], [dte], scale=-1.0)
        act(dte.t[:], dte.t[:], AF.Ln, [dte], [dte], bias=1.0)
        stt(dtv.t[:], dtr.t[:], 0.0, dte.t[:], ALU.max, ALU.add, [dtr, dte], [dtv])
        tt("dve", v3(dav.t[:], 4), v3(dtv.t[:], 4), ab.t[:, 0:8].unsqueeze(1).to_broadcast([128, 4, 8]), ALU.mult, [dtv, ab], [dav])
        yield
        def lnchain(i, ps_):
            v_ = vg[i % 2]
            ln = lnS[i % 2]
            vl_ = vl[i]
            act(v_.t[:], ps_.t[:], AF.Gelu_apprx_tanh, [ps_], [v_])
            op("dve", lambda e, v_=v_, ln=ln: e.tensor_reduce(out=ln.t[:, 0:8], in_=v3(v_.t[:], 8), axis=AX.X, op=ALU.add),
               [v_], [ln])
            act(sq.t[:], v_.t[:], AF.Square, [v_], [sq])
            op("dve", lambda e, ln=ln: e.tensor_reduce(out=ln.t[:, 8:16], in_=v3(sq.t[:], 8), axis=AX.X, op=ALU.add),
               [sq, ln], [ln])
            ts("dve", ln.t[:, 16:24], ln.t[:, 0:8], 1.0 / 64, ALU.mult, [ln], [ln])
            tt("dve", ln.t[:, 24:32], ln.t[:, 16:24], ln.t[:, 16:24], ALU.mult, [ln], [ln])
            stt(ln.t[:, 24:32], ln.t[:, 8:16], 1.0 / 64, ln.t[:, 24:32], ALU.mult, ALU.subtract, [ln], [ln])
            ts("pool", ln.t[:, 24:32], ln.t[:, 24:32], 1.0, ALU.mult, [ln], [ln], s2=EPS, op1=ALU.add)
            tt("pool", ln.t[:, 32:40], ln.t[:, 24:32], nh.t[:, 0:8], ALU.pow, [ln, nh], [ln])
            tt("dve", v3(vn.t[:], 8), v3(v_.t[:], 8), ln.t[:, 16:24].unsqueeze(2).to_broadcast([128, 8, 64]),
               ALU.subtract, [v_, ln], [vn])
            tt("dve", v3(vn.t[:], 8), v3(vn.t[:], 8), ln.t[:, 32:40].unsqueeze(2).to_broadcast([128, 8, 64]),
               ALU.mult, [vn, ln], [vn])
            tt("dve", vn.t[:], vn.t[:], rowb.t[:, R_LNW:R_LNW + 512], ALU.mult, [vn, rowb], [vn])
            tt("dve", vl_.t[:], vn.t[:], rowb.t[:, R_LNB:R_LNB + 512], ALU.add, [vn, rowb], [vl_])

        for blk in (1, 2, 0):
            slot = preload.pop((g, blk), None) or load_slot(blk)
            sl3 = v3(slot.t[:], 8)
            for i in range(4):
                ps_ = mmbank(4)
                pe([mm(ps_.t[:], hT3[:, k, i * 128:(i + 1) * 128], sl3[:, k, :], k == 0, k == 7) for k in range(8)],
                   [hT[i], slot], [ps_])
                if blk == 0:
                    act(mixt3[:, i, 0:512], ps_.t[:], AF.Gelu_apprx_tanh, [ps_], [mixt[i]])
                elif blk == 2:
                    act(mixt3[:, i, 512:1024], ps_.t[:], AF.Silu, [ps_], [mixt[i]])
                else:
                    lnchain(i, ps_)
                yield
        xslots = {}

        def projX(c):
            sl = 3 + c // 4
            cc = c % 4
            if cc == 0:
                xslots[sl] = load_slot(sl)
            slot = xslots[sl]
            sl3 = v3(slot.t[:], 8)
            ps_ = mmbank(4)
            pe([mm(ps_.t[:], sl3[:, k, cc * 128:(cc + 1) * 128], hT3[:, k, :], k == 0, k == 7) for k in range(8)],
               hT + [slot], [ps_])
            if seq_start:
                op("pool", lambda e, c=c: e.memset(xraw3[:, c, 0:4], 0.0), [], [xraw[c]])
            act(xraw3[:, c, 3:515], ps_.t[:], AF.Copy, [ps_], [xraw[c]])

        def convX(c):
            pc = mmbank(4)
            pe([mm(pc.t[:], cdg3[:, c * 4 + k, :], xraw3[:, c, k:k + 512], k == 0, k == 3) for k in range(4)],
               [cdg, xraw[c]], [pc])
            act(xact3[:, c, :], pc.t[:], AF.Silu, [pc, colp], [xact[c]], bias=colp.t[:, C_CB + c:C_CB + c + 1])
            cp("pool", xraw3[:, c, 0:3], xraw3[:, c, 512:515], [xraw[c]], [xraw[c]])

        projX(0)
        for c in range(8):
            if c + 1 < 8:
                projX(c + 1)
            yield
            convX(c)
        yield

        ck(3, mixt3[:, 0, 0:512], mixt)
        ck(5, xB3[:, 0, 0:512], xB)
        if seq_start:
            op("pool", lambda e: e.memset(state.t[:], 0.0), [], [state])
            op("pool", lambda e: e.memset(stbf.t[:], 0.0), [], [stbf])

        pa = mmbank(4)
        fns = []
        for i in range(4):
            da_i = dav.t[:, i * 8:(i + 1) * 8]
            fns.append(mm(pa.t[:, i * 16:i * 16 + 8], trif.t[:], da_i, True, True))
            fns.append(mm(pa.t[:, i * 16 + 8:i * 16 + 16], onesf.t[:], da_i, True, True))
        pe(fns, [trif, onesf, dav], [pa])
        ea_in, ea_out = ein4, eout4
        cp("dve", v3(ea_in.t[:], 4)[:, :, 0:16], v3(pa.t[:, 0:64], 4), [pa], [ea_in])
        tt("dve", v3(ea_in.t[:], 4)[:, :, 16:24], v3(ea_in.t[:], 4)[:, :, 8:16], v3(ea_in.t[:], 4)[:, :, 0:8], ALU.subtract, [ea_in], [ea_in])
        act(v3(ea_out.t[:], 4)[:, :, 0:24], v3(ea_in.t[:], 4)[:, :, 0:24], AF.Exp, [ea_in], [ea_out])
        tt("dve", v3(ea_out.t[:], 4)[:, :, 24:32], v3(ea_out.t[:], 4)[:, :, 16:24], v3(dtv.t[:], 4), ALU.mult, [ea_out, dtv], [ea_out])
        yield

        def ssd_front(i):
            tsl = slice(i * 128, (i + 1) * 128)
            R_, d_ = Rb2[i % 2], dec2[i % 2]
            for h in range(8):
                ts("pool", R_.t[:, h * 128:(h + 1) * 128], trib.t[:], dav.t[:, i * 8 + h:i * 8 + h + 1], ALU.mult, [trib, dav], [R_])
            pe([mm(M[2].t[:, gg * 128:(gg + 1) * 128], xact3[:, 4 + gg, tsl], xact3[:, 6 + gg, tsl], True, True) for gg in range(2)],
               xact[4:8], [M[2]])
            tt("dve", v3(cbm.t[:], 2), v3(M[2].t[:, 0:256], 2), trib.t[:].unsqueeze(1).to_broadcast([128, 2, 128]), ALU.mult,
               [M[2], trib], [cbm])
            pe([mm(M[0].t[:], m1b.t[:], R_.t[:, 0:512], True, True)], [m1b, R_], [M[0]])
            pe([mm(M[1].t[:], m1b.t[:], R_.t[:, 512:1024], True, True)], [m1b, R_], [M[1]])
            act(d_.t[:, 0:512], M[0].t[:], AF.Exp, [M[0]], [d_])
            act(d_.t[:, 512:1024], M[1].t[:], AF.Exp, [M[1]], [d_])

        def ssd_mid(i):
            R_, d_ = Rb2[i % 2], dec2[i % 2]
            for h in range(8):
                stt(R_.t[:, h * 128:(h + 1) * 128], d_.t[:, h * 128:(h + 1) * 128], dtv.t[:, i * 8 + h:i * 8 + h + 1],
                    cbm.t[:, (h // 4) * 128:(h // 4 + 1) * 128], ALU.mult, ALU.mult, [d_, dtv, cbm], [R_])
            tt("dve", v3(xw.t[:], 8), v3(xB3[:, i, 0:512], 8), ea_out.t[:, i * 32 + 24:i * 32 + 32].unsqueeze(2).to_broadcast([128, 8, 64]), ALU.mult,
               [xB[i], ea_out], [xw])

        def ssd_back(i):
            tsl = slice(i * 128, (i + 1) * 128)
            R_ = Rb2[i % 2]
            eo0 = i * 32
            fns = []
            for h in range(8):
                fns.append(mm(M[3].t[:, h * 64:(h + 1) * 64], R_.t[:, h * 128:(h + 1) * 128], xB3[:, i, h * 64:(h + 1) * 64], True, False))
                fns.append(mm(M[3].t[:, h * 64:(h + 1) * 64], dsk3[:, h, :], xB3[:, i, h * 64:(h + 1) * 64], False, True))
            pe(fns, [R_, xB[i], dsk], [M[3]])
            yo = M[0]
            pe([mm(yo.t[:, gg * 256:(gg + 1) * 256], xact3[:, 6 + gg, tsl], stbf.t[:, gg * 256:(gg + 1) * 256], True, True) for gg in range(2)],
               xact[6:8] + [stbf], [yo])
            sp_ = M[1]
            pe([mm(sp_.t[:, gg * 256:(gg + 1) * 256], xB3[:, i, 512 + gg * 128:512 + (gg + 1) * 128], xw.t[:, gg * 256:(gg + 1) * 256], True, True)
                for gg in range(2)], [xB[i], xw], [sp_])
            tt("dve", v3(state.t[:], 8), v3(state.t[:], 8), ea_out.t[:, eo0 + 8:eo0 + 16].unsqueeze(2).to_broadcast([128, 8, 64]), ALU.mult, [state, ea_out], [state])
            tt("dve", state.t[:], state.t[:], sp_.t[:], ALU.add, [state, sp_], [state])
            cp("act", stbf.t[:], state.t[:], [state], [stbf])
            tt("dve", v3(t1.t[:], 8), v3(yo.t[:], 8), ea_out.t[:, eo0:eo0 + 8].unsqueeze(2).to_broadcast([128, 8, 64]), ALU.mult, [yo, ea_out], [t1])
            tt("dve", t1.t[:], M[3].t[:], t1.t[:], ALU.add, [M[3], t1], [t1])
            tt("dve", t1.t[:], t1.t[:], mixt3[:, i, 512:1024], ALU.mult, [t1, mixt[i]], [t1])
            sq_ = ssq[i % 2]
            for gg in range(2):
                act(junk.t[:, 0:256], t1.t[:, gg * 256:(gg + 1) * 256], AF.Square, [t1], [junk, sq_], accum=sq_.t[:, gg:gg + 1])
            rsqrt_small(rg[i % 2].t[:, 0:2], sq_.t[:, 0:2], sq_.t[:, 2:4], 1.0 / 256, 2, sq_, sq_, rg[i % 2])
            for gg in range(2):
                stt(mixt3[:, i, 512 + gg * 256:512 + (gg + 1) * 256], t1.t[:, gg * 256:(gg + 1) * 256], rg[i % 2].t[:, gg:gg + 1],
                    rowb.t[:, R_SSMW + gg * 256:R_SSMW + (gg + 1) * 256], ALU.mult, ALU.mult, [t1, rg[i % 2], rowb], [mixt[i]])

        ssd_front(0)
        yield
        for i in range(4):
            pm = mmbank(4)
            fns = []
            for h in range(8):
                fns.append(mm(pm.t[:, h * 64:(h + 1) * 64], wsT3[:, h, :], vl[i].t[:, h * 64:(h + 1) * 64], True, False))
                fns.append(mm(pm.t[:, h * 64:(h + 1) * 64], bsb_t[0:1, h * 128:(h + 1) * 128], onesr_t[0:1, 0:64], False, True))
            pe(fns, [vl[i], wsT, bsb, onesr], [pm])
            tt("dve", mixt3[:, i, 0:512], pm.t[:], mixt3[:, i, 0:512], ALU.mult, [pm, mixt[i]], [mixt[i]])
            tb = tbank()
            pe([tr(tb.t[:, c * 128:(c + 1) * 128], xact3[:, c, i * 128:(i + 1) * 128]) for c in range(6)],
               xact[0:6] + [ident], [tb])
            cp("act", xB3[:, i, :], tb.t[:, 0:768], [tb], [xB[i]])
            yield
        for i in range(4):
            ssd_mid(i)
            yield
            if i + 1 < 4:
                ssd_front(i + 1)
                yield
            ssd_back(i)
            yield
        ck(6, mixt3[:, 0, 512:1024], mixt)

        s_o0 = load_slot(5)
        s_o1 = load_slot(6)
        so3 = [v3(s_o0.t[:], 8), v3(s_o1.t[:], 8)]
        so_t = [s_o0, s_o1]

        def f_proj(i):
            tb = tbank()
            pe([tr(tb.t[:, k * 128:(k + 1) * 128], mixt3[:, i, k * 128:(k + 1) * 128]) for k in range(8)], [mixt[i], ident], [tb])
            cp("act", hT3[:, :, i * 128:(i + 1) * 128], v3(tb.t[:], 8), [tb], [hT[i]])
            ob = [M[0], M[1]] if i % 2 == 0 else [M[2], M[3]]
            so_ = ssO[i]
            for n in range(2):
                pe([mm(ob[n].t[:], hT3[:, k, i * 128:(i + 1) * 128], so3[n][:, k, :], k == 0, k == 7) for k in range(8)],
                   [hT[i], so_t[n]], [ob[n]])
                act(junk.t[:, 0:512], ob[n].t[:], AF.Square, [ob[n]], [junk, so_], accum=so_.t[:, n:n + 1])

        def f_post(i):
            ob = [M[0], M[1]] if i % 2 == 0 else [M[2], M[3]]
            so_ = ssO[i]
            tt("dve", so_.t[:, 2:3], so_.t[:, 0:1], so_.t[:, 1:2], ALU.add, [so_], [so_])
            rsqrt_small(rsO[i].t[:, 0:1], so_.t[:, 2:3], so_.t[:, 3:4], 1.0 / 1024, 1, so_, so_, rsO[i])
            for n in range(2):
                t_ = tt_[n]
                stt(t_.t[:], ob[n].t[:], rsO[i].t[:, 0:1], rowb.t[:, R_POST + n * 512:R_POST + (n + 1) * 512], ALU.mult, ALU.mult,
                    [ob[n], rsO[i], rowb], [t_])
                tt("dve", xs[i].t[:, n * 512:(n + 1) * 512], xs[i].t[:, n * 512:(n + 1) * 512], t_.t[:], ALU.add, [xs[i], t_], [xs[i]])
            r0 = tok0 + i * 128
            dma("sp", out_d[r0:r0 + 128, :], xs[i].t[:], [xs[i]], [outT[i]])
            act(junk.t[:], xs[i].t[:], AF.Square, [xs[i]], [junk, ssF[i]], accum=ssF[i].t[:, 0:1])
            rsqrt_small(rsF[i].t[:, 0:1], ssF[i].t[:, 0:1], ssF[i].t[:, 1:2], 1.0 / 1024, 1, ssF[i], ssF[i], rsF[i])
            ts("dve", hb[i % 2].t[:], xs[i].t[:], rsF[i].t[:, 0:1], ALU.mult, [xs[i], rsF[i]], [hb[i % 2]])

        def f_tr(i):
            h_ = hb[i % 2]
            tb = tbank()
            pe([tr(tb.t[:, k * 128:(k + 1) * 128], h_.t[:, k * 128:(k + 1) * 128]) for k in range(8)], [h_, ident], [tb])
            tt("dve", h2T3[:, :, i * 128:(i + 1) * 128], v3(tb.t[:], 8), wpre_b(C_WFFN), ALU.mult, [tb, colp], [h2T[i]])

        f_proj(0)
        yield
        for i in range(4):
            if i + 1 < 4:
                f_proj(i + 1)
                if i + 1 == 3 and g + 1 < NG and stop == 0:
                    preload[(g + 1, 1)] = load_slot(1)
                    preload[(g + 1, 2)] = load_slot(2)
                yield
            f_post(i)
            yield
            if i >= 1:
                if i == 1:
                    yield "drain"
                f_tr(i - 1)
                yield
        f_tr(3)
        if g + 1 < NG and stop == 0:
            prodA(0, tok0 + 512)
            prodA(1, tok0 + 512)
            prodA1(2, tok0 + 512)
            prodA1(3, tok0 + 512)
            pre_issued[g + 1] = True
        yield
        ck(7, h2T3[:, 0, :], h2T)
        ffn_args[g] = outT

    ffn_args = {}
    frr = {"k": 0}

    def fbank():
        b = frr["k"] % 2
        frr["k"] += 1
        return Fb[b]

    def ffn(g):
        tok0 = g * 512
        outT = ffn_args[g]
        for j in range(8):
            slot = load_slot(7 + j, "f")
            sl3 = v3(slot.t[:], 8)
            for cc in range(4):
                f = j * 4 + cc
                ps_ = fbank()
                pe([mm(ps_.t[:], sl3[:, k, cc * 128:(cc + 1) * 128], h2T3[:, k, :], k == 0, k == 7) for k in range(8)],
                   h2T + [slot], [ps_])
                r_ = rl[f % 2]
                act(r_.t[:], ps_.t[:], AF.Relu, [ps_], [r_])
                tt("dve", fT3[:, f, :], r_.t[:], r_.t[:], ALU.mult, [r_], [fT[f]])
                yield
        ck(8, fT3[:, 0, :], fT)
        o2T3 = h2T3
        for dp in range(4):
            for qh in range(2):
                slot = load_slot(DOWN_BASE + dp * 2 + qh, "f")
                sl3 = v3(slot.t[:], 16)
                for db in range(2):
                    bank = Fb[db]
                    pe([mm(bank.t[:], sl3[:, kk, db * 128:(db + 1) * 128], fT3[:, qh * 16 + kk, :],
                           (qh == 0 and kk == 0), (qh == 1 and kk == 15)) for kk in range(16)],
                       fT[qh * 16:(qh + 1) * 16] + [slot], [bank])
                    if qh == 1:
                        act(o2T3[:, dp * 2 + db, :], bank.t[:], AF.Copy, [bank], h2T)
                    yield
        for i in range(4):
            tb = tbank()
            pe([tr(tb.t[:, k * 128:(k + 1) * 128], o2T3[:, k, i * 128:(i + 1) * 128]) for k in range(8)], [h2T[i], ident], [tb])
            sd_ = ssD[i]
            cp("dve", o2k.t[:], tb.t[:], [tb], [o2k])
            act(junk.t[:], o2k.t[:], AF.Square, [o2k], [junk, sd_], accum=sd_.t[:, 0:1])
            rsqrt_small(rsD[i].t[:, 0:1], sd_.t[:, 0:1], sd_.t[:, 1:2], 1.0 / 1024, 1, sd_, sd_, rsD[i])
            r0 = tok0 + i * 128
            for nn in range(2):
                t_ = ttF[nn]
                stt(t_.t[:], o2k.t[:, nn * 512:(nn + 1) * 512], rsD[i].t[:, 0:1], rowb.t[:, R_POST2 + nn * 512:R_POST2 + (nn + 1) * 512],
                    ALU.mult, ALU.mult, [o2k, rsD[i], rowb], [t_])
                dma("pool", out_d[r0:r0 + 128, nn * 512:(nn + 1) * 512], t_.t[:], [t_, outT[i]], [outT[i]], accum=True)
            yield

    def run_pair(a, b):
        live = [x for x in (a, b) if x is not None]
        credit = 0
        while live:
            for x in list(live):
                if x not in live:
                    continue
                steps = 1
                if x is b:
                    credit += 1
                    if credit % 2 == 0:
                        steps = 2
                for _ in range(steps):
                    if x not in live:
                        break
                    try:
                        r = next(x)
                    except StopIteration:
                        live.remove(x)
                        break
                    convert_some(1)
                    if r == "drain":
                        for y in list(live):
                            if y is not x:
                                for _ in y:
                                    pass
                                live.remove(y)

    try:
        prev = None
        for g in range(NG):
            run_pair(mixer(g), prev)
            prev = ffn(g)
        run_pair(prev, None)
    except _Stop:
        pass
    P.finish()
    P.emit()
    return nc


def _pack_weights(w_in, w_out, w_up, w_down):
    def slot(w, c0):
        blk = w[:, c0:c0 + 512].reshape(8, 128, 512).transpose(1, 0, 2)
        return np.ascontiguousarray(blk).reshape(128, 4096)

    sl = []
    for s in range(5):
        sl.append(slot(w_in, s * 512))
    for n in range(2):
        sl.append(slot(w_out, n * 512))
    for j in range(8):
        sl.append(slot(w_up, j * 512))
    for dp in range(4):
        for qh in range(2):
            blk = w_down[qh * 2048:(qh + 1) * 2048, dp * 256:(dp + 1) * 256].reshape(16, 128, 256).transpose(1, 0, 2)
            sl.append(np.ascontiguousarray(blk).reshape(128, 4096))
    return np.stack(sl, 0).astype(np.float32)


def make_in_maps(inputs, ncores, NG):
    x = np.asarray(inputs["x"], np.float32)
    w_in = np.asarray(inputs["w_in"], np.float32)[0]
    wsl = _pack_weights(w_in, np.asarray(inputs["w_out"], np.float32)[0], np.asarray(inputs["w_up"], np.float32)[0],
                        np.asarray(inputs["w_down"], np.float32)[0])
    wdt = np.ascontiguousarray(w_in[:, 2560:2568].reshape(8, 128, 8).transpose(1, 0, 2)).reshape(128, 64)
    g = lambda k: np.asarray(inputs[k], np.float32)[0]
    rowp = np.concatenate([g("norm_mix_post"), g("norm_ffn_post"), g("gm_ln_w").reshape(-1), g("gm_ln_b").reshape(-1),
                           g("ssm_norm_w"), g("dt_bias"), g("a_log"), g("d_skip")]).astype(np.float32)
    assert rowp.shape[0] == R_END
    colp = np.zeros((128, C_END), np.float32)
    colp[:, C_WPRE:C_WPRE + 8] = g("norm_mix_pre").reshape(8, 128).T
    colp[:, C_WFFN:C_WFFN + 8] = g("norm_ffn_pre").reshape(8, 128).T
    cw = g("conv_w")
    colp[:, C_CW:C_CW + 32] = cw.reshape(4, 8, 128).transpose(2, 1, 0).reshape(128, 32)
    colp[:, C_CB:C_CB + 8] = g("conv_b").reshape(8, 128).T
    ws = g("gm_w_s")
    wst = np.ascontiguousarray(ws.transpose(2, 0, 1)).reshape(128, 1024)
    bs = g("gm_b_s").reshape(1, 1024)
    xf = x.reshape(-1, 1024)
    ntok = NG * 512
    maps = []
    for c in range(ncores):
        maps.append({"x": np.ascontiguousarray(xf[c * ntok:(c + 1) * ntok]), "wslots": wsl, "wdt": wdt.astype(np.float32),
                     "rowpack": rowp, "colpack": colp, "wst": wst.astype(np.float32), "bsrow": bs.astype(np.float32)})
    return maps


def kernel(**inputs):
    NG = 8
    ncores = 8
    nc = build(NG)
    maps = make_in_maps(inputs, ncores, NG)
    res = run_bass_kernel_spmd(nc, maps, core_ids=list(range(ncores)))
    outs = [np.asarray(r["out"], np.float32) for r in res.results]
    return np.concatenate(outs, 0).reshape(16, 2048, 1024)
```

```python
from contextlib import ExitStack
import numpy as np
import os
XV = os.environ.get('XV', '')
import concourse.bass as bass
import concourse.mybir as mybir
from concourse.bass_utils import run_bass_kernel_spmd

F32 = mybir.dt.float32
BF16 = mybir.dt.bfloat16
AF = mybir.ActivationFunctionType
ALU = mybir.AluOpType
AX = mybir.AxisListType

EPS = 1e-6
NSLOT = 23
RING = 4
DOWN_BASE = 15
R_POST, R_POST2, R_LNW, R_LNB, R_SSMW, R_DTB, R_ALOG, R_DSK, R_END = 0, 1024, 2048, 2560, 3072, 3584, 3592, 3600, 3608
C_WPRE, C_WFFN, C_CW, C_CB, C_END = 0, 8, 16, 48, 56


class Tile:
    def __init__(self, t, name):
        self.t = t
        self.name = name
        self.w = None
        self.r = {}
        self.const = False
        self.psum = False


class Eng:
    def __init__(self, name, sem):
        self.name = name
        self.sem = sem
        self.cnt = 0
        self.seen = {}
        self.prog = []
        self.dsems = []
        self.dval = []
        self.dk = 0


class Planner:
    def __init__(self, nc, es):
        self.nc = nc
        self.es = es
        self.sems = {}
        self.E = {}
        for n in ("pe", "act", "dve", "pool", "sp"):
            s = es.enter_context(nc.semaphore("sem_" + n))
            self.sems[n] = s
            self.E[n] = Eng(n, s)
        for q, k in (("sp", 12), ("pool", 26)):
            for j in range(k):
                key = "d_%s_%d" % (q, j)
                self.sems[key] = es.enter_context(nc.semaphore(key))
                self.E[q].dsems.append(key)
                self.E[q].dval.append(0)
        self.nsb = 0

    def sb(self, name, cols, dt):
        t = self.es.enter_context(self.nc.sbuf_tensor("s_" + name, [128, cols], dt))
        return Tile(t, name)

    def sb_multi(self, name, cols, dt, nparts):
        t = self.es.enter_context(self.nc.sbuf_tensor("s_" + name, [128, cols], dt))
        return t, [Tile(t, "%s_%d" % (name, i)) for i in range(nparts)]

    def ps(self, name, cols, dt):
        t = self.es.enter_context(self.nc.psum_tensor("p_" + name, [128, cols], dt))
        tl = Tile(t, name)
        tl.psum = True
        return tl

    def _deps(self, E, reads, writes):
        deps = {}

        def add(tk):
            if tk is None:
                return
            k, v = tk
            if deps.get(k, 0) < v:
                deps[k] = v

        for t in reads:
            add(t.w)
            if t.psum:
                for k, v in t.r.items():
                    if k != E.name:
                        add((k, v))
        for t in writes:
            add(t.w)
            for k, v in t.r.items():
                add((k, v))
        for k, v in deps.items():
            if k == E.name and E.name == "pe":
                continue
            if E.seen.get(k, 0) >= v:
                continue
            E.seen[k] = v
            sem = self.sems[k]
            E.prog.append(lambda e, sem=sem, v=v: e.wait_ge(sem, v))

    def _record(self, tk, reads, writes):
        k, v = tk
        for t in reads:
            if not t.const:
                if t.r.get(k, 0) < v:
                    t.r[k] = v
        for t in writes:
            t.w = tk
            t.r = {}

    def op(self, eng, fn, reads=(), writes=()):
        E = self.E[eng]
        self._deps(E, reads, writes)
        E.cnt += 1
        sem = E.sem
        E.prog.append(lambda e, fn=fn, sem=sem: fn(e).then_inc(sem, 1))
        self._record((E.name, E.cnt), reads, writes)

    def pe(self, fns, reads=(), writes=(), inc=True):
        E = self.E["pe"]
        self._deps(E, reads, writes)
        if not inc:
            for fn in fns:
                E.prog.append(lambda e, fn=fn: fn(e))
            self._record((E.name, E.cnt + 1), reads, writes)
            return
        for fn in fns[:-1]:
            E.prog.append(lambda e, fn=fn: fn(e))
        E.cnt += 1
        sem = E.sem
        E.prog.append(lambda e, fn=fns[-1], sem=sem: fn(e).then_inc(sem, 1))
        self._record((E.name, E.cnt), reads, writes)

    def dma(self, q, out, in_, reads=(), writes=(), accum=False):
        E = self.E[q]
        self._deps(E, reads, writes)
        k = E.dk
        E.dk = (E.dk + 1) % len(E.dsems)
        key = E.dsems[k]
        sem = self.sems[key]
        prev = E.dval[k]
        if prev > 0 and E.seen.get(key, 0) < prev:
            E.seen[key] = prev
            E.prog.append(lambda e, sem=sem, v=prev: e.wait_ge(sem, v))
        E.dval[k] += 16
        if accum:
            E.prog.append(lambda e, out=out, in_=in_, sem=sem: e.dma_start(out=out, in_=in_, accum_op=ALU.add).then_inc(sem, 16))
        else:
            E.prog.append(lambda e, out=out, in_=in_, sem=sem: e.dma_start(out=out, in_=in_).then_inc(sem, 16))
        self._record((key, E.dval[k]), reads, writes)

    def finish(self):
        E = self.E["sp"]
        for q in ("sp", "pool"):
            Q = self.E[q]
            for key, v in zip(Q.dsems, Q.dval):
                if v > 0:
                    sem = self.sems[key]
                    E.prog.append(lambda e, sem=sem, v=v: e.wait_ge(sem, v))
        for n in ("pe", "act", "dve", "pool"):
            v = self.E[n].cnt
            if v > 0:
                sem = self.sems[n]
                E.prog.append(lambda e, sem=sem, v=v: e.wait_ge(sem, v))

    def emit(self):
        nc = self.nc
        E = self.E
        with nc.Block() as block:
            @block.sync
            def _(e):
                for f in E["sp"].prog:
                    f(e)

            @block.tensor
            def _(e):
                for f in E["pe"].prog:
                    f(e)

            @block.scalar
            def _(e):
                for f in E["act"].prog:
                    f(e)

            @block.vector
            def _(e):
                for f in E["dve"].prog:
                    f(e)

            @block.gpsimd
            def _(e):
                for f in E["pool"].prog:
                    f(e)


def v3(ap, a):
    return ap.rearrange("p (a b) -> p a b", a=a)


class _Stop(Exception):
    pass


def build(NG=8, stop=0):
    nc = bass.Bass("TRN2", target_bir_lowering=False)
    es = ExitStack()
    NT = NG * 512
    x_d = nc.dram_tensor("x", [NT, 1024], F32, kind="ExternalInput").ap()
    out_d = nc.dram_tensor("out", [NT, 1024], F32, kind="ExternalOutput").ap()
    wsl_d = nc.dram_tensor("wslots", [NSLOT, 128, 4096], F32, kind="ExternalInput").ap()
    wdt_d = nc.dram_tensor("wdt", [128, 64], F32, kind="ExternalInput").ap()
    rowp_d = nc.dram_tensor("rowpack", [R_END], F32, kind="ExternalInput").ap()
    colp_d = nc.dram_tensor("colpack", [128, C_END], F32, kind="ExternalInput").ap()
    wst_d = nc.dram_tensor("wst", [128, 1024], F32, kind="ExternalInput").ap()
    bs_d = nc.dram_tensor("bsrow", [1, 1024], F32, kind="ExternalInput").ap()
    scr_d = nc.dram_tensor("scr", [NSLOT, 128, 4096], BF16, kind="Internal").ap()

    P = Planner(nc, es)
    op, pe, dma = P.op, P.pe, P.dma

    ssD = [P.sb("ssD%d" % i, 4, F32) for i in range(4)]
    rsD = [P.sb("rsD%d" % i, 2, F32) for i in range(4)]
    fT_t, fT = P.sb_multi("fT", 32 * 512, BF16, 32)
    rowb = P.sb("rowb", R_END, F32)
    colp = P.sb("colp", C_END, F32)
    ident = P.sb("ident", 128, BF16)
    onesf = P.sb("onesf", 128, F32)
    trif = P.sb("trif", 128, F32)
    trib = P.sb("trib", 128, BF16)
    m1b = P.sb("m1b", 128, BF16)
    wsT = P.sb("wsT", 1024, BF16)
    dsk = P.sb("dsk", 1024, BF16)
    bsf = es.enter_context(nc.sbuf_tensor("s_bsf", [1, 1024], F32))
    bsb_t = es.enter_context(nc.sbuf_tensor("s_bsb", [1, 1024], BF16))
    bsb = Tile(bsb_t, "bsb")
    bsfT = Tile(bsf, "bsf")
    onesr_t = es.enter_context(nc.sbuf_tensor("s_onesr", [1, 64], BF16))
    onesr = Tile(onesr_t, "onesr")
    nh = P.sb("nh", 8, F32)
    ab = P.sb("ab", 8, F32)
    wdt = P.sb("wdt", 64, BF16)

    ring = [P.sb("ring%d" % r, 4096, BF16) for r in range(RING)]
    xs = [P.sb("xs%d" % i, 1024, F32) for i in range(4)]
    hb = [P.sb("hb%d" % i, 1024, BF16) for i in range(2)]
    junk = P.sb("junk", 1024, BF16)
    hT_t, hT = P.sb_multi("hT", 4096, BF16, 4)
    h2T_t, h2T = P.sb_multi("h2T", 4096, BF16, 4)
    vg = [P.sb("vg0", 512, F32)] * 2
    sq = P.sb("sq", 512, F32)
    vn = P.sb("vn", 512, F32)
    vl = [P.sb("vl%d" % i, 512, BF16) for i in range(4)]
    xraw_t, xraw = P.sb_multi("xraw", 8 * 516, BF16, 8)
    cdg = P.sb("cdg", 32 * 128, BF16)
    xact_t, xact = P.sb_multi("xact", 4096, BF16, 8)
    xB_t, xB = P.sb_multi("xB", 4 * 768, BF16, 4)
    mixt_t, mixt = P.sb_multi("mixt", 4096, BF16, 4)
    Rb2 = [P.sb("Rb%d" % i, 1024, BF16) for i in range(2)]
    dec2 = [P.sb("dec%d" % i, 1024, BF16) for i in range(1)] * 2
    cbm = P.sb("cbm", 256, BF16)
    t1 = P.sb("t1", 512, F32)
    xw = P.sb("xw", 512, BF16)
    state = P.sb("state", 512, F32)
    stbf = P.sb("stbf", 512, BF16)
    tt_ = [P.sb("tt%d" % i, 512, F32) for i in range(2)]
    rl = [P.sb("rl%d" % i, 512, F32) for i in range(2)]
    ttF = [P.sb("ttF%d" % i, 512, F32) for i in range(2)]
    o2k = P.sb("o2k", 1024, BF16)

    def smalls(name, n, cnt):
        return [P.sb("%s%d" % (name, i), n, F32) for i in range(cnt)]

    ssA = smalls("ssA", 2, 4)
    rsA = smalls("rsA", 2, 4)
    lnS = smalls("lnS", 48, 2)
    dtr = P.sb("dtr", 32, F32)
    dta = P.sb("dta", 32, F32)
    dte = P.sb("dte", 32, F32)
    dtv = P.sb("dtv", 32, F32)
    dav = P.sb("dav", 32, F32)
    ein4 = P.sb("ein4", 128, F32)
    eout4 = P.sb("eout4", 128, F32)
    ssq = smalls("ssq", 4, 2)
    rg = smalls("rg", 2, 2)
    ssO = smalls("ssO", 4, 4)
    rsO = smalls("rsO", 2, 4)
    ssF = smalls("ssF", 2, 4)
    rsF = smalls("rsF", 2, 4)

    pb = [P.ps("pb%d" % i, 512, F32) for i in range(6)]
    ptb = [P.ps("ptb%d" % i, 1024, BF16) for i in range(2)]

    scrT = [Tile(None, "scr%d" % s) for s in range(NSLOT)]

    def A(tile):
        return tile.t[:]

    def act(out, in_, func, reads, writes, bias=None, scale=None, accum=None):
        kw = {}
        if bias is not None:
            kw["bias"] = bias
        if scale is not None:
            kw["scale"] = scale
        if accum is not None:
            kw["accum_out"] = accum
        op("act", lambda e: e.activation(out=out, in_=in_, func=func, **kw), reads, writes)

    def ts(eng, out, in0, s1, op0, reads, writes, s2=None, op1=None):
        if op1 is None and eng == "pool":
            op(eng, lambda e: e.tensor_scalar(out=out, in0=in0, scalar1=s1, scalar2=1.0, op0=op0, op1=ALU.mult), reads, writes)
        elif op1 is None:
            op(eng, lambda e: e.tensor_scalar(out=out, in0=in0, scalar1=s1, scalar2=None, op0=op0), reads, writes)
        else:
            op(eng, lambda e: e.tensor_scalar(out=out, in0=in0, scalar1=s1, scalar2=s2, op0=op0, op1=op1), reads, writes)

    def tt(eng, out, in0, in1, o, reads, writes):
        op(eng, lambda e: e.tensor_tensor(out=out, in0=in0, in1=in1, op=o), reads, writes)

    def stt(out, in0, scalar, in1, op0, op1, reads, writes):
        op("dve", lambda e: e.scalar_tensor_tensor(out=out, in0=in0, scalar=scalar, in1=in1, op0=op0, op1=op1), reads, writes)

    def cp(eng, out, in_, reads, writes):
        if eng == "act":
            op(eng, lambda e: e.activation(out=out, in_=in_, func=AF.Copy), reads, writes)
        else:
            op(eng, lambda e: e.tensor_copy(out=out, in_=in_), reads, writes)

    def mm(out, lhsT, rhs, start, stop):
        return lambda e: e.matmul(out=out, lhsT=lhsT, rhs=rhs, start=start, stop=stop)

    def tr(out, in_):
        idn = ident.t[:]
        return lambda e: e.transpose(out=out, in_=in_, identity=idn)

    def rsqrt_small(dst, src, tmp, scale, n, reads_t, tmp_t, dst_t):
        ts("pool", tmp, src, scale, ALU.mult, [reads_t], [tmp_t], s2=EPS, op1=ALU.add)
        tt("pool", dst, tmp, nh.t[:, 0:n], ALU.pow, [tmp_t, nh], [dst_t])

    ring_state = {"m": 0, "f": 0}

    def load_slot(s, pool="m"):
        k = ring_state[pool]
        ring_state[pool] = k + 1
        r = (k % 2) + (0 if pool == "m" else 2)
        assert scrT[s].w is not None, "weight slot %d used before its conversion was issued" % s
        dma("sp", ring[r].t[:], scr_d[s], reads=[scrT[s]], writes=[ring[r]])
        return ring[r]

    def ck(n, src_ap=None, reads=()):
        if stop != n:
            return
        if src_ap is not None:
            ncol = src_ap.shape[-1]
            cp("dve", vn.t[:, 0:ncol], src_ap, list(reads), [vn])
            dma("sp", out_d[0:128, 0:ncol], vn.t[:, 0:ncol], [vn], [])
        raise _Stop()

    dma("sp", rowb.t[:], rowp_d.partition_broadcast(128), [], [rowb])
    dma("sp", colp.t[:], colp_d, [], [colp])
    dma("sp", xs[0].t[:], wst_d, [], [xs[0]])
    dma("sp", bsf[:], bs_d, [], [bsfT])
    dma("pool", wdt.t[:], wdt_d, [], [wdt])
    op("pool", lambda e: e.memset(onesf.t[:], 1.0), [], [onesf])
    op("pool", lambda e: e.memset(nh.t[:], -0.5), [], [nh])
    op("pool", lambda e: e.memset(onesr_t[:], 1.0), [], [onesr])
    op("pool", lambda e: e.affine_select(out=ident.t[:], in_=onesf.t[:], pattern=[[-1, 128]], compare_op=ALU.is_equal,
                                          fill=0.0, base=0, channel_multiplier=1), [onesf], [ident])
    op("pool", lambda e: e.affine_select(out=trif.t[:], in_=onesf.t[:], pattern=[[1, 128]], compare_op=ALU.is_ge,
                                          fill=0.0, base=0, channel_multiplier=-1), [onesf], [trif])
    op("pool", lambda e: e.affine_select(out=trib.t[:], in_=onesf.t[:], pattern=[[1, 128]], compare_op=ALU.is_ge,
                                          fill=0.0, base=0, channel_multiplier=-1), [onesf], [trib])
    op("pool", lambda e: e.affine_select(out=m1b.t[:], in_=onesf.t[:], pattern=[[-1, 128]], compare_op=ALU.is_gt,
                                          fill=0.0, base=0, channel_multiplier=1), [onesf], [m1b])
    conv_order = [1, 2, 0, 3, 4, 5, 6] + list(range(7, NSLOT))
    conv_state = {"k": 0}

    def convert_some(n):
        for _ in range(n):
            k = conv_state["k"]
            if k >= NSLOT:
                return
            sl_ = conv_order[k]
            conv_state["k"] = k + 1
            dma("pool", scr_d[sl_], wsl_d[sl_], [], [scrT[sl_]])

    convert_some(5)
    tt("dve", v3(wsT.t[:], 8), v3(xs[0].t[:], 8), trif.t[:].unsqueeze(1).to_broadcast([128, 8, 128]), ALU.mult,
       [xs[0], trif], [wsT])
    cp("dve", bsb_t[:], bsf[:], [bsfT], [bsb])
    act(ab.t[:], rowb.t[:, R_ALOG:R_ALOG + 8], AF.Exp, [rowb], [ab])
    ts("dve", ab.t[:], ab.t[:], -1.0, ALU.mult, [ab], [ab])
    for h in range(8):
        ts("dve", dsk.t[:, h * 128:(h + 1) * 128], ident.t[:], rowb.t[:, R_DSK + h:R_DSK + h + 1], ALU.mult,
           [ident, rowb], [dsk])
    for j in range(32):
        ts("dve", cdg.t[:, j * 128:(j + 1) * 128], ident.t[:], colp.t[:, C_CW + j:C_CW + j + 1], ALU.mult, [ident, colp], [cdg])
    for t_ in (rowb, colp, ident, onesf, trif, trib, m1b, wsT, dsk, bsb, onesr, nh, ab, wdt, cdg):
        t_.const = True
    for t_ in scrT:
        t_.const = True

    try:
        ck(1, wsT.t[:, 0:512], [wsT])
    except _Stop:
        P.finish(); P.emit(); return nc
    hT3 = v3(hT_t[:], 8)
    h2T3 = v3(h2T_t[:], 8)
    xraw3 = v3(xraw_t[:], 8)
    cdg3 = v3(cdg.t[:], 32)
    xact3 = v3(xact_t[:], 8)
    xB3 = v3(xB_t[:], 4)
    mixt3 = v3(mixt_t[:], 4)
    fT3 = v3(fT_t[:], 32)
    wsT3 = v3(wsT.t[:], 8)
    dsk3 = v3(dsk.t[:], 8)
    wdt3 = v3(wdt.t[:], 8)

    def wpre_b(off):
        return colp.t[:, off:off + 8].unsqueeze(2).to_broadcast([128, 8, 128])

    bank_rr = {"k": 0}

    def mmbank(n):
        b = bank_rr["k"] % n
        bank_rr["k"] += 1
        return pb[b]

    tb_rr = {"k": 0}

    def tbank():
        b = tb_rr["k"] % 2
        tb_rr["k"] += 1
        return ptb[b]

    M = pb[0:4]
    Fb = pb[4:6]

    pre_issued = {}
    preload = {}

    def mixer(g):
        tok0 = g * 512
        seq_start = (g % 4 == 0)
        outT = [Tile(None, "out_%d_%d" % (g, i)) for i in range(4)]

        W = 0.5
        def prodA1(i, base=None):
            r0 = (tok0 if base is None else base) + i * 128
            dma("sp", xs[i].t[:], x_d[r0:r0 + 128, :], [], [xs[i]])
            act(junk.t[:], xs[i].t[:], AF.Square, [xs[i]], [junk, ssA[i]], accum=ssA[i].t[:, 0:1])
            rsqrt_small(rsA[i].t[:, 0:1], ssA[i].t[:, 0:1], ssA[i].t[:, 1:2], 1.0 / 1024, 1, ssA[i], ssA[i], rsA[i])

        def prodA(i, base=None, front=True):
            if front:
                prodA1(i, base)
            ts("dve", hb[i % 2].t[:], xs[i].t[:], rsA[i].t[:, 0:1], ALU.mult, [xs[i], rsA[i]], [hb[i % 2]])

        def consA(i):
            h_ = hb[i % 2]
            tb = tbank()
            pe([tr(tb.t[:, k * 128:(k + 1) * 128], h_.t[:, k * 128:(k + 1) * 128]) for k in range(8)], [h_, ident], [tb])
            tt("dve", hT3[:, :, i * 128:(i + 1) * 128], v3(tb.t[:], 8), wpre_b(C_WPRE), ALU.mult, [tb, colp], [hT[i]])

        if not pre_issued.get(g):
            prodA(0)
            prodA(1)
            yield W
            yield W
        for i in range(4):
            consA(i)
            if i + 2 < 4:
                prodA(i + 2, front=not pre_issued.get(g))
            yield W
        ck(2, hT3[:, 0, :], hT)

        pdt = mmbank(4)
        for i in range(4):
            pe([mm(pdt.t[:, i * 8:(i + 1) * 8], hT3[:, k, i * 128:(i + 1) * 128], wdt3[:, k, :], k == 0, k == 7) for k in range(8)],
               [hT[i], wdt], [pdt])
        tt("dve", v3(dtr.t[:], 4), v3(pdt.t[:, 0:32], 4), rowb.t[:, R_DTB:R_DTB + 8].unsqueeze(1).to_broadcast([128, 4, 8]),
           ALU.add, [pdt, rowb], [dtr])
        stt(dta.t[:], dtr.t[:], -1.0, dtr.t[:], ALU.mult, ALU.max, [dtr], [dta])
        act(dte.t[:], dta.t[:], AF.Exp, [dta], [dte], scale=-1.0)
        act(dte.t[:], dte.t[:], AF.Ln, [dte], [dte], bias=1.0)
        stt(dtv.t[:], dtr.t[:], 0.0, dte.t[:], ALU.max, ALU.add, [dtr, dte], [dtv])
        tt("dve", v3(dav.t[:], 4), v3(dtv.t[:], 4), ab.t[:, 0:8].unsqueeze(1).to_broadcast([128, 4, 8]), ALU.mult, [dtv, ab], [dav])
        yield W
        def lnchain(i, ps_):
            v_ = vg[i % 2]
            ln = lnS[i % 2]
            vl_ = vl[i]
            act(v_.t[:], ps_.t[:], AF.Gelu_apprx_tanh, [ps_], [v_])
            op("dve", lambda e, v_=v_, ln=ln: e.tensor_reduce(out=ln.t[:, 0:8], in_=v3(v_.t[:], 8), axis=AX.X, op=ALU.add),
               [v_], [ln])
            act(sq.t[:], v_.t[:], AF.Square, [v_], [sq])
            op("dve", lambda e, ln=ln: e.tensor_reduce(out=ln.t[:, 8:16], in_=v3(sq.t[:], 8), axis=AX.X, op=ALU.add),
               [sq, ln], [ln])
            ts("dve", ln.t[:, 16:24], ln.t[:, 0:8], 1.0 / 64, ALU.mult, [ln], [ln])
            tt("dve", ln.t[:, 24:32], ln.t[:, 16:24], ln.t[:, 16:24], ALU.mult, [ln], [ln])
            stt(ln.t[:, 24:32], ln.t[:, 8:16], 1.0 / 64, ln.t[:, 24:32], ALU.mult, ALU.subtract, [ln], [ln])
            ts("pool", ln.t[:, 24:32], ln.t[:, 24:32], 1.0, ALU.mult, [ln], [ln], s2=EPS, op1=ALU.add)
            tt("pool", ln.t[:, 32:40], ln.t[:, 24:32], nh.t[:, 0:8], ALU.pow, [ln, nh], [ln])
            tt("dve", v3(vn.t[:], 8), v3(v_.t[:], 8), ln.t[:, 16:24].unsqueeze(2).to_broadcast([128, 8, 64]),
               ALU.subtract, [v_, ln], [vn])
            tt("dve", v3(vn.t[:], 8), v3(vn.t[:], 8), ln.t[:, 32:40].unsqueeze(2).to_broadcast([128, 8, 64]),
               ALU.mult, [vn, ln], [vn])
            tt("dve", vn.t[:], vn.t[:], rowb.t[:, R_LNW:R_LNW + 512], ALU.mult, [vn, rowb], [vn])
            tt("dve", vl_.t[:], vn.t[:], rowb.t[:, R_LNB:R_LNB + 512], ALU.add, [vn, rowb], [vl_])

        for blk in (1, 2, 0):
            slot = preload.pop((g, blk), None) or load_slot(blk)
            sl3 = v3(slot.t[:], 8)
            for i in range(4):
                ps_ = mmbank(4)
                pe([mm(ps_.t[:], hT3[:, k, i * 128:(i + 1) * 128], sl3[:, k, :], k == 0, k == 7) for k in range(8)],
                   [hT[i], slot], [ps_])
                if blk == 0:
                    act(mixt3[:, i, 0:512], ps_.t[:], AF.Gelu_apprx_tanh, [ps_], [mixt[i]])
                elif blk == 2:
                    act(mixt3[:, i, 512:1024], ps_.t[:], AF.Silu, [ps_], [mixt[i]])
                else:
                    lnchain(i, ps_)
                yield W
        xslots = {}

        def projX(c):
            sl = 3 + c // 4
            cc = c % 4
            if cc == 0:
                xslots[sl] = load_slot(sl)
            slot = xslots[sl]
            sl3 = v3(slot.t[:], 8)
            ps_ = mmbank(4)
            pe([mm(ps_.t[:], sl3[:, k, cc * 128:(cc + 1) * 128], hT3[:, k, :], k == 0, k == 7) for k in range(8)],
               hT + [slot], [ps_])
            if seq_start:
                op("pool", lambda e, c=c: e.memset(xraw3[:, c, 0:4], 0.0), [], [xraw[c]])
            act(xraw3[:, c, 3:515], ps_.t[:], AF.Copy, [ps_], [xraw[c]])

        def convX(c):
            pc = mmbank(4)
            pe([mm(pc.t[:], cdg3[:, c * 4 + k, :], xraw3[:, c, k:k + 512], k == 0, k == 3) for k in range(4)],
               [cdg, xraw[c]], [pc])
            act(xact3[:, c, :], pc.t[:], AF.Silu, [pc, colp], [xact[c]], bias=colp.t[:, C_CB + c:C_CB + c + 1])
            cp("pool", xraw3[:, c, 0:3], xraw3[:, c, 512:515], [xraw[c]], [xraw[c]])

        projX(0)
        for c in range(8):
            if c + 1 < 8:
                projX(c + 1)
            yield W
            convX(c)
        yield W

        ck(3, mixt3[:, 0, 0:512], mixt)
        ck(5, xB3[:, 0, 0:512], xB)
        if seq_start:
            op("pool", lambda e: e.memset(state.t[:], 0.0), [], [state])
            op("pool", lambda e: e.memset(stbf.t[:], 0.0), [], [stbf])

        W = 1.5
        pa = mmbank(4)
        fns = []
        for i in range(4):
            da_i = dav.t[:, i * 8:(i + 1) * 8]
            fns.append(mm(pa.t[:, i * 16:i * 16 + 8], trif.t[:], da_i, True, True))
            fns.append(mm(pa.t[:, i * 16 + 8:i * 16 + 16], onesf.t[:], da_i, True, True))
        pe(fns, [trif, onesf, dav], [pa])
        ea_in, ea_out = ein4, eout4
        cp("dve", v3(ea_in.t[:], 4)[:, :, 0:16], v3(pa.t[:, 0:64], 4), [pa], [ea_in])
        tt("dve", v3(ea_in.t[:], 4)[:, :, 16:24], v3(ea_in.t[:], 4)[:, :, 8:16], v3(ea_in.t[:], 4)[:, :, 0:8], ALU.subtract, [ea_in], [ea_in])
        act(v3(ea_out.t[:], 4)[:, :, 0:24], v3(ea_in.t[:], 4)[:, :, 0:24], AF.Exp, [ea_in], [ea_out])
        tt("dve", v3(ea_out.t[:], 4)[:, :, 24:32], v3(ea_out.t[:], 4)[:, :, 16:24], v3(dtv.t[:], 4), ALU.mult, [ea_out, dtv], [ea_out])
        yield W

        def ssd_front(i):
            tsl = slice(i * 128, (i + 1) * 128)
            R_, d_ = Rb2[i % 2], dec2[i % 2]
            for h in range(8):
                ts("pool", R_.t[:, h * 128:(h + 1) * 128], trib.t[:], dav.t[:, i * 8 + h:i * 8 + h + 1], ALU.mult, [trib, dav], [R_])
            pe([mm(M[2].t[:, gg * 128:(gg + 1) * 128], xact3[:, 4 + gg, tsl], xact3[:, 6 + gg, tsl], True, True) for gg in range(2)],
               xact[4:8], [M[2]])
            tt("dve", v3(cbm.t[:], 2), v3(M[2].t[:, 0:256], 2), trib.t[:].unsqueeze(1).to_broadcast([128, 2, 128]), ALU.mult,
               [M[2], trib], [cbm])
            pe([mm(M[0].t[:], m1b.t[:], R_.t[:, 0:512], True, True)], [m1b, R_], [M[0]])
            pe([mm(M[1].t[:], m1b.t[:], R_.t[:, 512:1024], True, True)], [m1b, R_], [M[1]])
            act(d_.t[:, 0:512], M[0].t[:], AF.Exp, [M[0]], [d_])
            act(d_.t[:, 512:1024], M[1].t[:], AF.Exp, [M[1]], [d_])

        def ssd_mid(i):
            R_, d_ = Rb2[i % 2], dec2[i % 2]
            for h in range(8):
                stt(R_.t[:, h * 128:(h + 1) * 128], d_.t[:, h * 128:(h + 1) * 128], dtv.t[:, i * 8 + h:i * 8 + h + 1],
                    cbm.t[:, (h // 4) * 128:(h // 4 + 1) * 128], ALU.mult, ALU.mult, [d_, dtv, cbm], [R_])
            tt("dve", v3(xw.t[:], 8), v3(xB3[:, i, 0:512], 8), ea_out.t[:, i * 32 + 24:i * 32 + 32].unsqueeze(2).to_broadcast([128, 8, 64]), ALU.mult,
               [xB[i], ea_out], [xw])

        def ssd_back(i):
            tsl = slice(i * 128, (i + 1) * 128)
            R_ = Rb2[i % 2]
            eo0 = i * 32
            fns = []
            for h in range(8):
                fns.append(mm(M[3].t[:, h * 64:(h + 1) * 64], R_.t[:, h * 128:(h + 1) * 128], xB3[:, i, h * 64:(h + 1) * 64], True, False))
                fns.append(mm(M[3].t[:, h * 64:(h + 1) * 64], dsk3[:, h, :], xB3[:, i, h * 64:(h + 1) * 64], False, True))
            pe(fns, [R_, xB[i], dsk], [M[3]])
            yo = M[0]
            pe([mm(yo.t[:, gg * 256:(gg + 1) * 256], xact3[:, 6 + gg, tsl], stbf.t[:, gg * 256:(gg + 1) * 256], True, True) for gg in range(2)],
               xact[6:8] + [stbf], [yo])
            sp_ = M[1]
            pe([mm(sp_.t[:, gg * 256:(gg + 1) * 256], xB3[:, i, 512 + gg * 128:512 + (gg + 1) * 128], xw.t[:, gg * 256:(gg + 1) * 256], True, True)
                for gg in range(2)], [xB[i], xw], [sp_])
            tt("dve", v3(state.t[:], 8), v3(state.t[:], 8), ea_out.t[:, eo0 + 8:eo0 + 16].unsqueeze(2).to_broadcast([128, 8, 64]), ALU.mult, [state, ea_out], [state])
            tt("dve", state.t[:], state.t[:], sp_.t[:], ALU.add, [state, sp_], [state])
            cp("act", stbf.t[:], state.t[:], [state], [stbf])
            tt("dve", v3(t1.t[:], 8), v3(yo.t[:], 8), ea_out.t[:, eo0:eo0 + 8].unsqueeze(2).to_broadcast([128, 8, 64]), ALU.mult, [yo, ea_out], [t1])
            tt("dve", t1.t[:], M[3].t[:], t1.t[:], ALU.add, [M[3], t1], [t1])
            tt("dve", t1.t[:], t1.t[:], mixt3[:, i, 512:1024], ALU.mult, [t1, mixt[i]], [t1])
            sq_ = ssq[i % 2]
            for gg in range(2):
                act(junk.t[:, 0:256], t1.t[:, gg * 256:(gg + 1) * 256], AF.Square, [t1], [junk, sq_], accum=sq_.t[:, gg:gg + 1])
            rsqrt_small(rg[i % 2].t[:, 0:2], sq_.t[:, 0:2], sq_.t[:, 2:4], 1.0 / 256, 2, sq_, sq_, rg[i % 2])
            for gg in range(2):
                stt(mixt3[:, i, 512 + gg * 256:512 + (gg + 1) * 256], t1.t[:, gg * 256:(gg + 1) * 256], rg[i % 2].t[:, gg:gg + 1],
                    rowb.t[:, R_SSMW + gg * 256:R_SSMW + (gg + 1) * 256], ALU.mult, ALU.mult, [t1, rg[i % 2], rowb], [mixt[i]])

        ssd_front(0)
        yield W
        for i in range(4):
            pm = mmbank(4)
            fns = []
            for h in range(8):
                fns.append(mm(pm.t[:, h * 64:(h + 1) * 64], wsT3[:, h, :], vl[i].t[:, h * 64:(h + 1) * 64], True, False))
                fns.append(mm(pm.t[:, h * 64:(h + 1) * 64], bsb_t[0:1, h * 128:(h + 1) * 128], onesr_t[0:1, 0:64], False, True))
            pe(fns, [vl[i], wsT, bsb, onesr], [pm])
            tt("dve", mixt3[:, i, 0:512], pm.t[:], mixt3[:, i, 0:512], ALU.mult, [pm, mixt[i]], [mixt[i]])
            tb = tbank()
            pe([tr(tb.t[:, c * 128:(c + 1) * 128], xact3[:, c, i * 128:(i + 1) * 128]) for c in range(6)],
               xact[0:6] + [ident], [tb])
            cp("act", xB3[:, i, :], tb.t[:, 0:768], [tb], [xB[i]])
            yield W
        for i in range(4):
            ssd_mid(i)
            yield W
            if i + 1 < 4:
                ssd_front(i + 1)
                yield W
            ssd_back(i)
            yield W
        ck(6, mixt3[:, 0, 512:1024], mixt)

        s_o0 = load_slot(5)
        s_o1 = load_slot(6)
        so3 = [v3(s_o0.t[:], 8), v3(s_o1.t[:], 8)]
        so_t = [s_o0, s_o1]

        def f_proj(i):
            tb = tbank()
            pe([tr(tb.t[:, k * 128:(k + 1) * 128], mixt3[:, i, k * 128:(k + 1) * 128]) for k in range(8)], [mixt[i], ident], [tb])
            cp("act", hT3[:, :, i * 128:(i + 1) * 128], v3(tb.t[:], 8), [tb], [hT[i]])
            ob = [M[0], M[1]] if i % 2 == 0 else [M[2], M[3]]
            so_ = ssO[i]
            for n in range(2):
                pe([mm(ob[n].t[:], hT3[:, k, i * 128:(i + 1) * 128], so3[n][:, k, :], k == 0, k == 7) for k in range(8)],
                   [hT[i], so_t[n]], [ob[n]])
                act(junk.t[:, 0:512], ob[n].t[:], AF.Square, [ob[n]], [junk, so_], accum=so_.t[:, n:n + 1])

        def f_post(i):
            ob = [M[0], M[1]] if i % 2 == 0 else [M[2], M[3]]
            so_ = ssO[i]
            tt("dve", so_.t[:, 2:3], so_.t[:, 0:1], so_.t[:, 1:2], ALU.add, [so_], [so_])
            rsqrt_small(rsO[i].t[:, 0:1], so_.t[:, 2:3], so_.t[:, 3:4], 1.0 / 1024, 1, so_, so_, rsO[i])
            for n in range(2):
                t_ = tt_[n]
                stt(t_.t[:], ob[n].t[:], rsO[i].t[:, 0:1], rowb.t[:, R_POST + n * 512:R_POST + (n + 1) * 512], ALU.mult, ALU.mult,
                    [ob[n], rsO[i], rowb], [t_])
                tt("dve", xs[i].t[:, n * 512:(n + 1) * 512], xs[i].t[:, n * 512:(n + 1) * 512], t_.t[:], ALU.add, [xs[i], t_], [xs[i]])
            r0 = tok0 + i * 128
            dma("sp", out_d[r0:r0 + 128, :], xs[i].t[:], [xs[i]], [outT[i]])
            act(junk.t[:], xs[i].t[:], AF.Square, [xs[i]], [junk, ssF[i]], accum=ssF[i].t[:, 0:1])
            rsqrt_small(rsF[i].t[:, 0:1], ssF[i].t[:, 0:1], ssF[i].t[:, 1:2], 1.0 / 1024, 1, ssF[i], ssF[i], rsF[i])
            ts("dve", hb[i % 2].t[:], xs[i].t[:], rsF[i].t[:, 0:1], ALU.mult, [xs[i], rsF[i]], [hb[i % 2]])

        def f_tr(i):
            h_ = hb[i % 2]
            tb = tbank()
            pe([tr(tb.t[:, k * 128:(k + 1) * 128], h_.t[:, k * 128:(k + 1) * 128]) for k in range(8)], [h_, ident], [tb])
            tt("dve", h2T3[:, :, i * 128:(i + 1) * 128], v3(tb.t[:], 8), wpre_b(C_WFFN), ALU.mult, [tb, colp], [h2T[i]])

        f_proj(0)
        yield W
        for i in range(4):
            if i + 1 < 4:
                f_proj(i + 1)
                if i + 1 == 3 and g + 1 < NG and stop == 0:
                    preload[(g + 1, 1)] = load_slot(1)
                    preload[(g + 1, 2)] = load_slot(2)
                yield W
            f_post(i)
            yield W
            if i >= 1:
                if i == 1:
                    yield "drain"
                f_tr(i - 1)
                yield W
        f_tr(3)
        if g + 1 < NG and stop == 0:
            prodA(0, tok0 + 512)
            prodA(1, tok0 + 512)
            prodA1(2, tok0 + 512)
            prodA1(3, tok0 + 512)
            pre_issued[g + 1] = True
        yield W
        ck(7, h2T3[:, 0, :], h2T)
        ffn_args[g] = outT

    ffn_args = {}
    frr = {"k": 0}

    def fbank():
        b = frr["k"] % 2
        frr["k"] += 1
        return Fb[b]

    def ffn(g):
        tok0 = g * 512
        outT = ffn_args[g]
        for j in range(8):
            slot = load_slot(7 + j, "f")
            sl3 = v3(slot.t[:], 8)
            for cc in range(4):
                f = j * 4 + cc
                ps_ = fbank()
                pe([mm(ps_.t[:], sl3[:, k, cc * 128:(cc + 1) * 128], h2T3[:, k, :], k == 0, k == 7) for k in range(8)],
                   h2T + [slot], [ps_])
                r_ = rl[f % 2]
                act(r_.t[:], ps_.t[:], AF.Relu, [ps_], [r_])
                tt("dve", fT3[:, f, :], r_.t[:], r_.t[:], ALU.mult, [r_], [fT[f]])
                yield
        ck(8, fT3[:, 0, :], fT)
        o2T3 = h2T3
        for dp in range(4):
            for qh in range(2):
                slot = load_slot(DOWN_BASE + dp * 2 + qh, "f")
                sl3 = v3(slot.t[:], 16)
                for db in range(2):
                    bank = Fb[db]
                    pe([mm(bank.t[:], sl3[:, kk, db * 128:(db + 1) * 128], fT3[:, qh * 16 + kk, :],
                           (qh == 0 and kk == 0), (qh == 1 and kk == 15)) for kk in range(16)],
                       fT[qh * 16:(qh + 1) * 16] + [slot], [bank])
                    if qh == 1:
                        act(o2T3[:, dp * 2 + db, :], bank.t[:], AF.Copy, [bank], h2T)
                    yield
        for i in range(4):
            tb = tbank()
            pe([tr(tb.t[:, k * 128:(k + 1) * 128], o2T3[:, k, i * 128:(i + 1) * 128]) for k in range(8)], [h2T[i], ident], [tb])
            sd_ = ssD[i]
            cp("dve", o2k.t[:], tb.t[:], [tb], [o2k])
            act(junk.t[:], o2k.t[:], AF.Square, [o2k], [junk, sd_], accum=sd_.t[:, 0:1])
            rsqrt_small(rsD[i].t[:, 0:1], sd_.t[:, 0:1], sd_.t[:, 1:2], 1.0 / 1024, 1, sd_, sd_, rsD[i])
            r0 = tok0 + i * 128
            for nn in range(2):
                t_ = ttF[nn]
                stt(t_.t[:], o2k.t[:, nn * 512:(nn + 1) * 512], rsD[i].t[:, 0:1], rowb.t[:, R_POST2 + nn * 512:R_POST2 + (nn + 1) * 512],
                    ALU.mult, ALU.mult, [o2k, rsD[i], rowb], [t_])
                dma("pool", out_d[r0:r0 + 128, nn * 512:(nn + 1) * 512], t_.t[:], [t_, outT[i]], [outT[i]], accum=True)
            yield

    def run_pair(a, b):
        budget = 0.0
        a_live, b_live = a is not None, b is not None
        while a_live or b_live:
            if a_live:
                try:
                    r = next(a)
                except StopIteration:
                    a_live = False
                    r = None
                convert_some(1)
                if r == "drain":
                    while b_live:
                        try:
                            next(b)
                        except StopIteration:
                            b_live = False
                elif isinstance(r, float):
                    budget += r
            else:
                budget += 1.0
            while b_live and budget >= 1.0:
                budget -= 1.0
                try:
                    next(b)
                except StopIteration:
                    b_live = False
            if not b_live:
                budget = 0.0

    try:
        prev = None
        for g in range(NG):
            run_pair(mixer(g), prev)
            prev = ffn(g)
        run_pair(None, prev)
    except _Stop:
        pass
    P.finish()
    P.emit()
    return nc


def _pack_weights(w_in, w_out, w_up, w_down):
    def slot(w, c0):
        blk = w[:, c0:c0 + 512].reshape(8, 128, 512).transpose(1, 0, 2)
        return np.ascontiguousarray(blk).reshape(128, 4096)

    sl = []
    for s in range(5):
        sl.append(slot(w_in, s * 512))
    for n in range(2):
        sl.append(slot(w_out, n * 512))
    for j in range(8):
        sl.append(slot(w_up, j * 512))
    for dp in range(4):
        for qh in range(2):
            blk = w_down[qh * 2048:(qh + 1) * 2048, dp * 256:(dp + 1) * 256].reshape(16, 128, 256).transpose(1, 0, 2)
            sl.append(np.ascontiguousarray(blk).reshape(128, 4096))
    return np.stack(sl, 0).astype(np.float32)


def make_in_maps(inputs, ncores, NG):
    x = np.asarray(inputs["x"], np.float32)
    w_in = np.asarray(inputs["w_in"], np.float32)[0]
    wsl = _pack_weights(w_in, np.asarray(inputs["w_out"], np.float32)[0], np.asarray(inputs["w_up"], np.float32)[0],
                        np.asarray(inputs["w_down"], np.float32)[0])
    wdt = np.ascontiguousarray(w_in[:, 2560:2568].reshape(8, 128, 8).transpose(1, 0, 2)).reshape(128, 64)
    g = lambda k: np.asarray(inputs[k], np.float32)[0]
    rowp = np.concatenate([g("norm_mix_post"), g("norm_ffn_post"), g("gm_ln_w").reshape(-1), g("gm_ln_b").reshape(-1),
                           g("ssm_norm_w"), g("dt_bias"), g("a_log"), g("d_skip")]).astype(np.float32)
    assert rowp.shape[0] == R_END
    colp = np.zeros((128, C_END), np.float32)
    colp[:, C_WPRE:C_WPRE + 8] = g("norm_mix_pre").reshape(8, 128).T
    colp[:, C_WFFN:C_WFFN + 8] = g("norm_ffn_pre").reshape(8, 128).T
    cw = g("conv_w")
    colp[:, C_CW:C_CW + 32] = cw.reshape(4, 8, 128).transpose(2, 1, 0).reshape(128, 32)
    colp[:, C_CB:C_CB + 8] = g("conv_b").reshape(8, 128).T
    ws = g("gm_w_s")
    wst = np.ascontiguousarray(ws.transpose(2, 0, 1)).reshape(128, 1024)
    bs = g("gm_b_s").reshape(1, 1024)
    xf = x.reshape(-1, 1024)
    ntok = NG * 512
    maps = []
    for c in range(ncores):
        maps.append({"x": np.ascontiguousarray(xf[c * ntok:(c + 1) * ntok]), "wslots": wsl, "wdt": wdt.astype(np.float32),
                     "rowpack": rowp, "colpack": colp, "wst": wst.astype(np.float32), "bsrow": bs.astype(np.float32)})
    return maps


def kernel(**inputs):
    NG = 8
    ncores = 8
    nc = build(NG)
    maps = make_in_maps(inputs, ncores, NG)
    res = run_bass_kernel_spmd(nc, maps, core_ids=list(range(ncores)))
    outs = [np.asarray(r["out"], np.float32) for r in res.results]
    return np.concatenate(outs, 0).reshape(16, 2048, 1024)
```

```python
from contextlib import ExitStack
import numpy as np
import os
XV = os.environ.get('XV', '')
import concourse.bass as bass
import concourse.mybir as mybir
from concourse.bass_utils import run_bass_kernel_spmd

F32 = mybir.dt.float32
BF16 = mybir.dt.bfloat16
AF = mybir.ActivationFunctionType
ALU = mybir.AluOpType
AX = mybir.AxisListType

EPS = 1e-6
NSLOT = 23
RING = 4
DOWN_BASE = 15
R_POST, R_POST2, R_LNW, R_LNB, R_SSMW, R_DTB, R_ALOG, R_DSK, R_END = 0, 1024, 2048, 2560, 3072, 3584, 3592, 3600, 3608
C_WPRE, C_WFFN, C_CW, C_CB, C_END = 0, 8, 16, 48, 56


class Tile:
    def __init__(self, t, name):
        self.t = t
        self.name = name
        self.w = None
        self.r = {}
        self.const = False
        self.psum = False


class Eng:
    def __init__(self, name, sem):
        self.name = name
        self.sem = sem
        self.cnt = 0
        self.seen = {}
        self.prog = []
        self.dsems = []
        self.dval = []
        self.dk = 0


class Planner:
    def __init__(self, nc, es):
        self.nc = nc
        self.es = es
        self.sems = {}
        self.E = {}
        for n in ("pe", "act", "dve", "pool", "sp"):
            s = es.enter_context(nc.semaphore("sem_" + n))
            self.sems[n] = s
            self.E[n] = Eng(n, s)
        for q, k in (("sp", 12), ("pool", 26)):
            for j in range(k):
                key = "d_%s_%d" % (q, j)
                self.sems[key] = es.enter_context(nc.semaphore(key))
                self.E[q].dsems.append(key)
                self.E[q].dval.append(0)
        self.nsb = 0

    def sb(self, name, cols, dt):
        t = self.es.enter_context(self.nc.sbuf_tensor("s_" + name, [128, cols], dt))
        return Tile(t, name)

    def sb_multi(self, name, cols, dt, nparts):
        t = self.es.enter_context(self.nc.sbuf_tensor("s_" + name, [128, cols], dt))
        return t, [Tile(t, "%s_%d" % (name, i)) for i in range(nparts)]

    def ps(self, name, cols, dt):
        t = self.es.enter_context(self.nc.psum_tensor("p_" + name, [128, cols], dt))
        tl = Tile(t, name)
        tl.psum = True
        return tl

    def _deps(self, E, reads, writes):
        deps = {}

        def add(tk):
            if tk is None:
                return
            k, v = tk
            if deps.get(k, 0) < v:
                deps[k] = v

        for t in reads:
            add(t.w)
            if t.psum:
                for k, v in t.r.items():
                    if k != E.name:
                        add((k, v))
        for t in writes:
            add(t.w)
            for k, v in t.r.items():
                add((k, v))
        for k, v in deps.items():
            if k == E.name and E.name == "pe":
                continue
            if E.seen.get(k, 0) >= v:
                continue
            E.seen[k] = v
            sem = self.sems[k]
            E.prog.append(lambda e, sem=sem, v=v: e.wait_ge(sem, v))

    def _record(self, tk, reads, writes):
        k, v = tk
        for t in reads:
            if not t.const:
                if t.r.get(k, 0) < v:
                    t.r[k] = v
        for t in writes:
            t.w = tk
            t.r = {}

    def op(self, eng, fn, reads=(), writes=()):
        E = self.E[eng]
        self._deps(E, reads, writes)
        E.cnt += 1
        sem = E.sem
        E.prog.append(lambda e, fn=fn, sem=sem: fn(e).then_inc(sem, 1))
        self._record((E.name, E.cnt), reads, writes)

    def pe(self, fns, reads=(), writes=(), inc=True):
        E = self.E["pe"]
        self._deps(E, reads, writes)
        if not inc:
            for fn in fns:
                E.prog.append(lambda e, fn=fn: fn(e))
            self._record((E.name, E.cnt + 1), reads, writes)
            return
        for fn in fns[:-1]:
            E.prog.append(lambda e, fn=fn: fn(e))
        E.cnt += 1
        sem = E.sem
        E.prog.append(lambda e, fn=fns[-1], sem=sem: fn(e).then_inc(sem, 1))
        self._record((E.name, E.cnt), reads, writes)

    def dma(self, q, out, in_, reads=(), writes=(), accum=False):
        E = self.E[q]
        self._deps(E, reads, writes)
        k = E.dk
        E.dk = (E.dk + 1) % len(E.dsems)
        key = E.dsems[k]
        sem = self.sems[key]
        prev = E.dval[k]
        if prev > 0 and E.seen.get(key, 0) < prev:
            E.seen[key] = prev
            E.prog.append(lambda e, sem=sem, v=prev: e.wait_ge(sem, v))
        E.dval[k] += 16
        if accum:
            E.prog.append(lambda e, out=out, in_=in_, sem=sem: e.dma_start(out=out, in_=in_, accum_op=ALU.add).then_inc(sem, 16))
        else:
            E.prog.append(lambda e, out=out, in_=in_, sem=sem: e.dma_start(out=out, in_=in_).then_inc(sem, 16))
        self._record((key, E.dval[k]), reads, writes)

    def finish(self):
        E = self.E["sp"]
        for q in ("sp", "pool"):
            Q = self.E[q]
            for key, v in zip(Q.dsems, Q.dval):
                if v > 0:
                    sem = self.sems[key]
                    E.prog.append(lambda e, sem=sem, v=v: e.wait_ge(sem, v))
        for n in ("pe", "act", "dve", "pool"):
            v = self.E[n].cnt
            if v > 0:
                sem = self.sems[n]
                E.prog.append(lambda e, sem=sem, v=v: e.wait_ge(sem, v))

    def emit(self):
        nc = self.nc
        E = self.E
        with nc.Block() as block:
            @block.sync
            def _(e):
                for f in E["sp"].prog:
                    f(e)

            @block.tensor
            def _(e):
                for f in E["pe"].prog:
                    f(e)

            @block.scalar
            def _(e):
                for f in E["act"].prog:
                    f(e)

            @block.vector
            def _(e):
                for f in E["dve"].prog:
                    f(e)

            @block.gpsimd
            def _(e):
                for f in E["pool"].prog:
                    f(e)


def v3(ap, a):
    return ap.rearrange("p (a b) -> p a b", a=a)


class _Stop(Exception):
    pass


def build(NG=8, stop=0):
    nc = bass.Bass("TRN2", target_bir_lowering=False)
    es = ExitStack()
    NT = NG * 512
    x_d = nc.dram_tensor("x", [NT, 1024], F32, kind="ExternalInput").ap()
    out_d = nc.dram_tensor("out", [NT, 1024], F32, kind="ExternalOutput").ap()
    wsl_d = nc.dram_tensor("wslots", [NSLOT, 128, 4096], F32, kind="ExternalInput").ap()
    wdt_d = nc.dram_tensor("wdt", [128, 64], F32, kind="ExternalInput").ap()
    rowp_d = nc.dram_tensor("rowpack", [R_END], F32, kind="ExternalInput").ap()
    colp_d = nc.dram_tensor("colpack", [128, C_END], F32, kind="ExternalInput").ap()
    wst_d = nc.dram_tensor("wst", [128, 1024], F32, kind="ExternalInput").ap()
    bs_d = nc.dram_tensor("bsrow", [1, 1024], F32, kind="ExternalInput").ap()
    scr_d = nc.dram_tensor("scr", [NSLOT, 128, 4096], BF16, kind="Internal").ap()

    P = Planner(nc, es)
    op, pe, dma = P.op, P.pe, P.dma

    ssD = [P.sb("ssD%d" % i, 4, F32) for i in range(4)]
    rsD = [P.sb("rsD%d" % i, 2, F32) for i in range(4)]
    fT_t, fT = P.sb_multi("fT", 32 * 512, BF16, 32)
    rowb = P.sb("rowb", R_END, F32)
    colp = P.sb("colp", C_END, F32)
    ident = P.sb("ident", 128, BF16)
    onesf = P.sb("onesf", 128, F32)
    trif = P.sb("trif", 128, F32)
    trib = P.sb("trib", 128, BF16)
    m1b = P.sb("m1b", 128, BF16)
    wsT = P.sb("wsT", 1024, BF16)
    dsk = P.sb("dsk", 1024, BF16)
    bsf = es.enter_context(nc.sbuf_tensor("s_bsf", [1, 1024], F32))
    bsb_t = es.enter_context(nc.sbuf_tensor("s_bsb", [1, 1024], BF16))
    bsb = Tile(bsb_t, "bsb")
    bsfT = Tile(bsf, "bsf")
    onesr_t = es.enter_context(nc.sbuf_tensor("s_onesr", [1, 64], BF16))
    onesr = Tile(onesr_t, "onesr")
    nh = P.sb("nh", 8, F32)
    ab = P.sb("ab", 8, F32)
    wdt = P.sb("wdt", 64, BF16)

    ring = [P.sb("ring%d" % r, 4096, BF16) for r in range(RING)]
    xs = [P.sb("xs%d" % i, 1024, F32) for i in range(4)]
    hb = [P.sb("hb%d" % i, 1024, BF16) for i in range(2)]
    junk = P.sb("junk", 1024, BF16)
    hT_t, hT = P.sb_multi("hT", 4096, BF16, 4)
    h2T_t, h2T = P.sb_multi("h2T", 4096, BF16, 4)
    vg = [P.sb("vg0", 512, F32)] * 2
    sq = P.sb("sq", 512, F32)
    vn = P.sb("vn", 512, F32)
    vl = [P.sb("vl%d" % i, 512, BF16) for i in range(4)]
    xraw_t, xraw = P.sb_multi("xraw", 8 * 516, BF16, 8)
    cdg = P.sb("cdg", 32 * 128, BF16)
    xact_t, xact = P.sb_multi("xact", 4096, BF16, 8)
    xB_t, xB = P.sb_multi("xB", 4 * 768, BF16, 4)
    mixt_t, mixt = P.sb_multi("mixt", 4096, BF16, 4)
    Rb2 = [P.sb("Rb%d" % i, 1024, BF16) for i in range(2)]
    dec2 = [P.sb("dec%d" % i, 1024, BF16) for i in range(1)] * 2
    cbm = P.sb("cbm", 256, BF16)
    t1 = P.sb("t1", 512, F32)
    xw = P.sb("xw", 512, BF16)
    state = P.sb("state", 512, F32)
    stbf = P.sb("stbf", 512, BF16)
    tt_ = [P.sb("tt%d" % i, 512, F32) for i in range(2)]
    rl = [P.sb("rl%d" % i, 512, F32) for i in range(2)]
    ttF = [P.sb("ttF%d" % i, 512, F32) for i in range(2)]
    o2k = P.sb("o2k", 1024, BF16)

    def smalls(name, n, cnt):
        return [P.sb("%s%d" % (name, i), n, F32) for i in range(cnt)]

    ssA = smalls("ssA", 2, 4)
    rsA = smalls("rsA", 2, 4)
    lnS = smalls("lnS", 48, 2)
    dtr = P.sb("dtr", 32, F32)
    dta = P.sb("dta", 32, F32)
    dte = P.sb("dte", 32, F32)
    dtv = P.sb("dtv", 32, F32)
    dav = P.sb("dav", 32, F32)
    ein4 = P.sb("ein4", 128, F32)
    eout4 = P.sb("eout4", 128, F32)
    ssq = smalls("ssq", 4, 2)
    rg = smalls("rg", 2, 2)
    ssO = smalls("ssO", 4, 4)
    rsO = smalls("rsO", 2, 4)
    ssF = smalls("ssF", 2, 4)
    rsF = smalls("rsF", 2, 4)

    pb = [P.ps("pb%d" % i, 512, F32) for i in range(6)]
    ptb = [P.ps("ptb%d" % i, 1024, BF16) for i in range(2)]

    scrT = [Tile(None, "scr%d" % s) for s in range(NSLOT)]

    def A(tile):
        return tile.t[:]

    def act(out, in_, func, reads, writes, bias=None, scale=None, accum=None):
        kw = {}
        if bias is not None:
            kw["bias"] = bias
        if scale is not None:
            kw["scale"] = scale
        if accum is not None:
            kw["accum_out"] = accum
        op("act", lambda e: e.activation(out=out, in_=in_, func=func, **kw), reads, writes)

    def ts(eng, out, in0, s1, op0, reads, writes, s2=None, op1=None):
        if op1 is None and eng == "pool":
            op(eng, lambda e: e.tensor_scalar(out=out, in0=in0, scalar1=s1, scalar2=1.0, op0=op0, op1=ALU.mult), reads, writes)
        elif op1 is None:
            op(eng, lambda e: e.tensor_scalar(out=out, in0=in0, scalar1=s1, scalar2=None, op0=op0), reads, writes)
        else:
            op(eng, lambda e: e.tensor_scalar(out=out, in0=in0, scalar1=s1, scalar2=s2, op0=op0, op1=op1), reads, writes)

    def tt(eng, out, in0, in1, o, reads, writes):
        op(eng, lambda e: e.tensor_tensor(out=out, in0=in0, in1=in1, op=o), reads, writes)

    def stt(out, in0, scalar, in1, op0, op1, reads, writes):
        op("dve", lambda e: e.scalar_tensor_tensor(out=out, in0=in0, scalar=scalar, in1=in1, op0=op0, op1=op1), reads, writes)

    def cp(eng, out, in_, reads, writes):
        if eng == "act":
            op(eng, lambda e: e.activation(out=out, in_=in_, func=AF.Copy), reads, writes)
        else:
            op(eng, lambda e: e.tensor_copy(out=out, in_=in_), reads, writes)

    def mm(out, lhsT, rhs, start, stop):
        return lambda e: e.matmul(out=out, lhsT=lhsT, rhs=rhs, start=start, stop=stop)

    def tr(out, in_):
        idn = ident.t[:]
        return lambda e: e.transpose(out=out, in_=in_, identity=idn)

    def rsqrt_small(dst, src, tmp, scale, n, reads_t, tmp_t, dst_t):
        ts("pool", tmp, src, scale, ALU.mult, [reads_t], [tmp_t], s2=EPS, op1=ALU.add)
        tt("pool", dst, tmp, nh.t[:, 0:n], ALU.pow, [tmp_t, nh], [dst_t])

    ring_state = {"m": 0, "f": 0}

    def load_slot(s, pool="m"):
        k = ring_state[pool]
        ring_state[pool] = k + 1
        r = (k % 2) + (0 if pool == "m" else 2)
        assert scrT[s].w is not None, "weight slot %d used before its conversion was issued" % s
        dma("sp", ring[r].t[:], scr_d[s], reads=[scrT[s]], writes=[ring[r]])
        return ring[r]

    def ck(n, src_ap=None, reads=()):
        if stop != n:
            return
        if src_ap is not None:
            ncol = src_ap.shape[-1]
            cp("dve", vn.t[:, 0:ncol], src_ap, list(reads), [vn])
            dma("sp", out_d[0:128, 0:ncol], vn.t[:, 0:ncol], [vn], [])
        raise _Stop()

    dma("sp", rowb.t[:], rowp_d.partition_broadcast(128), [], [rowb])
    dma("sp", colp.t[:], colp_d, [], [colp])
    dma("sp", xs[0].t[:], wst_d, [], [xs[0]])
    dma("sp", bsf[:], bs_d, [], [bsfT])
    dma("pool", wdt.t[:], wdt_d, [], [wdt])
    op("pool", lambda e: e.memset(onesf.t[:], 1.0), [], [onesf])
    op("pool", lambda e: e.memset(nh.t[:], -0.5), [], [nh])
    op("pool", lambda e: e.memset(onesr_t[:], 1.0), [], [onesr])
    op("pool", lambda e: e.affine_select(out=ident.t[:], in_=onesf.t[:], pattern=[[-1, 128]], compare_op=ALU.is_equal,
                                          fill=0.0, base=0, channel_multiplier=1), [onesf], [ident])
    op("pool", lambda e: e.affine_select(out=trif.t[:], in_=onesf.t[:], pattern=[[1, 128]], compare_op=ALU.is_ge,
                                          fill=0.0, base=0, channel_multiplier=-1), [onesf], [trif])
    op("pool", lambda e: e.affine_select(out=trib.t[:], in_=onesf.t[:], pattern=[[1, 128]], compare_op=ALU.is_ge,
                                          fill=0.0, base=0, channel_multiplier=-1), [onesf], [trib])
    op("pool", lambda e: e.affine_select(out=m1b.t[:], in_=onesf.t[:], pattern=[[-1, 128]], compare_op=ALU.is_gt,
                                          fill=0.0, base=0, channel_multiplier=1), [onesf], [m1b])
    conv_order = [1, 2, 0, 3, 4, 5, 6] + list(range(7, NSLOT))
    conv_state = {"k": 0}

    def convert_some(n):
        for _ in range(n):
            k = conv_state["k"]
            if k >= NSLOT:
                return
            sl_ = conv_order[k]
            conv_state["k"] = k + 1
            dma("pool", scr_d[sl_], wsl_d[sl_], [], [scrT[sl_]])

    convert_some(5)
    tt("dve", v3(wsT.t[:], 8), v3(xs[0].t[:], 8), trif.t[:].unsqueeze(1).to_broadcast([128, 8, 128]), ALU.mult,
       [xs[0], trif], [wsT])
    cp("dve", bsb_t[:], bsf[:], [bsfT], [bsb])
    act(ab.t[:], rowb.t[:, R_ALOG:R_ALOG + 8], AF.Exp, [rowb], [ab])
    ts("dve", ab.t[:], ab.t[:], -1.0, ALU.mult, [ab], [ab])
    for h in range(8):
        ts("dve", dsk.t[:, h * 128:(h + 1) * 128], ident.t[:], rowb.t[:, R_DSK + h:R_DSK + h + 1], ALU.mult,
           [ident, rowb], [dsk])
    for j in range(32):
        ts("dve", cdg.t[:, j * 128:(j + 1) * 128], ident.t[:], colp.t[:, C_CW + j:C_CW + j + 1], ALU.mult, [ident, colp], [cdg])
    for t_ in (rowb, colp, ident, onesf, trif, trib, m1b, wsT, dsk, bsb, onesr, nh, ab, wdt, cdg):
        t_.const = True
    for t_ in scrT:
        t_.const = True

    try:
        ck(1, wsT.t[:, 0:512], [wsT])
    except _Stop:
        P.finish(); P.emit(); return nc
    hT3 = v3(hT_t[:], 8)
    h2T3 = v3(h2T_t[:], 8)
    xraw3 = v3(xraw_t[:], 8)
    cdg3 = v3(cdg.t[:], 32)
    xact3 = v3(xact_t[:], 8)
    xB3 = v3(xB_t[:], 4)
    mixt3 = v3(mixt_t[:], 4)
    fT3 = v3(fT_t[:], 32)
    wsT3 = v3(wsT.t[:], 8)
    dsk3 = v3(dsk.t[:], 8)
    wdt3 = v3(wdt.t[:], 8)

    def wpre_b(off):
        return colp.t[:, off:off + 8].unsqueeze(2).to_broadcast([128, 8, 128])

    bank_rr = {"k": 0}

    def mmbank(n):
        b = bank_rr["k"] % n
        bank_rr["k"] += 1
        return pb[b]

    tb_rr = {"k": 0}

    def tbank():
        b = tb_rr["k"] % 2
        tb_rr["k"] += 1
        return ptb[b]

    M = pb[0:4]
    Fb = pb[4:6]

    pre_issued = {}
    preload = {}

    def mixer(g):
        tok0 = g * 512
        seq_start = (g % 4 == 0)
        outT = [Tile(None, "out_%d_%d" % (g, i)) for i in range(4)]

        W = 0.5
        def prodA1(i, base=None):
            r0 = (tok0 if base is None else base) + i * 128
            dma("sp", xs[i].t[:], x_d[r0:r0 + 128, :], [], [xs[i]])
            act(junk.t[:], xs[i].t[:], AF.Square, [xs[i]], [junk, ssA[i]], accum=ssA[i].t[:, 0:1])
            rsqrt_small(rsA[i].t[:, 0:1], ssA[i].t[:, 0:1], ssA[i].t[:, 1:2], 1.0 / 1024, 1, ssA[i], ssA[i], rsA[i])

        def prodA(i, base=None, front=True):
            if front:
                prodA1(i, base)
            ts("dve", hb[i % 2].t[:], xs[i].t[:], rsA[i].t[:, 0:1], ALU.mult, [xs[i], rsA[i]], [hb[i % 2]])

        def consA(i):
            h_ = hb[i % 2]
            tb = tbank()
            pe([tr(tb.t[:, k * 128:(k + 1) * 128], h_.t[:, k * 128:(k + 1) * 128]) for k in range(8)], [h_, ident], [tb])
            tt("dve", hT3[:, :, i * 128:(i + 1) * 128], v3(tb.t[:], 8), wpre_b(C_WPRE), ALU.mult, [tb, colp], [hT[i]])

        if not pre_issued.get(g):
            prodA(0)
            prodA(1)
            yield W
            yield W
        for i in range(4):
            consA(i)
            if i + 2 < 4:
                prodA(i + 2, front=not pre_issued.get(g))
            yield W
        ck(2, hT3[:, 0, :], hT)

        pdt = mmbank(4)
        for i in range(4):
            pe([mm(pdt.t[:, i * 8:(i + 1) * 8], hT3[:, k, i * 128:(i + 1) * 128], wdt3[:, k, :], k == 0, k == 7) for k in range(8)],
               [hT[i], wdt], [pdt])
        tt("dve", v3(dtr.t[:], 4), v3(pdt.t[:, 0:32], 4), rowb.t[:, R_DTB:R_DTB + 8].unsqueeze(1).to_broadcast([128, 4, 8]),
           ALU.add, [pdt, rowb], [dtr])
        stt(dta.t[:], dtr.t[:], -1.0, dtr.t[:], ALU.mult, ALU.max, [dtr], [dta])
        act(dte.t[:], dta.t[:], AF.Exp, [dta], [dte], scale=-1.0)
        act(dte.t[:], dte.t[:], AF.Ln, [dte], [dte], bias=1.0)
        stt(dtv.t[:], dtr.t[:], 0.0, dte.t[:], ALU.max, ALU.add, [dtr, dte], [dtv])
        tt("dve", v3(dav.t[:], 4), v3(dtv.t[:], 4), ab.t[:, 0:8].unsqueeze(1).to_broadcast([128, 4, 8]), ALU.mult, [dtv, ab], [dav])
        yield W
        def lnchain(i, ps_):
            v_ = vg[i % 2]
            ln = lnS[i % 2]
            vl_ = vl[i]
            act(v_.t[:], ps_.t[:], AF.Gelu_apprx_tanh, [ps_], [v_])
            op("dve", lambda e, v_=v_, ln=ln: e.tensor_reduce(out=ln.t[:, 0:8], in_=v3(v_.t[:], 8), axis=AX.X, op=ALU.add),
               [v_], [ln])
            act(sq.t[:], v_.t[:], AF.Square, [v_], [sq])
            op("dve", lambda e, ln=ln: e.tensor_reduce(out=ln.t[:, 8:16], in_=v3(sq.t[:], 8), axis=AX.X, op=ALU.add),
               [sq, ln], [ln])
            ts("dve", ln.t[:, 16:24], ln.t[:, 0:8], 1.0 / 64, ALU.mult, [ln], [ln])
            tt("dve", ln.t[:, 24:32], ln.t[:, 16:24], ln.t[:, 16:24], ALU.mult, [ln], [ln])
            stt(ln.t[:, 24:32], ln.t[:, 8:16], 1.0 / 64, ln.t[:, 24:32], ALU.mult, ALU.subtract, [ln], [ln])
            ts("pool", ln.t[:, 24:32], ln.t[:, 24:32], 1.0, ALU.mult, [ln], [ln], s2=EPS, op1=ALU.add)
            tt("pool", ln.t[:, 32:40], ln.t[:, 24:32], nh.t[:, 0:8], ALU.pow, [ln, nh], [ln])
            tt("dve", v3(vn.t[:], 8), v3(v_.t[:], 8), ln.t[:, 16:24].unsqueeze(2).to_broadcast([128, 8, 64]),
               ALU.subtract, [v_, ln], [vn])
            tt("dve", v3(vn.t[:], 8), v3(vn.t[:], 8), ln.t[:, 32:40].unsqueeze(2).to_broadcast([128, 8, 64]),
               ALU.mult, [vn, ln], [vn])
            tt("dve", vn.t[:], vn.t[:], rowb.t[:, R_LNW:R_LNW + 512], ALU.mult, [vn, rowb], [vn])
            tt("dve", vl_.t[:], vn.t[:], rowb.t[:, R_LNB:R_LNB + 512], ALU.add, [vn, rowb], [vl_])

        for blk in (1, 2, 0):
            slot = preload.pop((g, blk), None) or load_slot(blk)
            sl3 = v3(slot.t[:], 8)
            for i in range(4):
                ps_ = mmbank(4)
                pe([mm(ps_.t[:], hT3[:, k, i * 128:(i + 1) * 128], sl3[:, k, :], k == 0, k == 7) for k in range(8)],
                   [hT[i], slot], [ps_])
                if blk == 0:
                    act(mixt3[:, i, 0:512], ps_.t[:], AF.Gelu_apprx_tanh, [ps_], [mixt[i]])
                elif blk == 2:
                    act(mixt3[:, i, 512:1024], ps_.t[:], AF.Silu, [ps_], [mixt[i]])
                else:
                    lnchain(i, ps_)
                yield W
        xslots = {}

        def projX(c):
            sl = 3 + c // 4
            cc = c % 4
            if cc == 0:
                xslots[sl] = load_slot(sl)
            slot = xslots[sl]
            sl3 = v3(slot.t[:], 8)
            ps_ = mmbank(4)
            pe([mm(ps_.t[:], sl3[:, k, cc * 128:(cc + 1) * 128], hT3[:, k, :], k == 0, k == 7) for k in range(8)],
               hT + [slot], [ps_])
            if seq_start:
                op("pool", lambda e, c=c: e.memset(xraw3[:, c, 0:4], 0.0), [], [xraw[c]])
            act(xraw3[:, c, 3:515], ps_.t[:], AF.Copy, [ps_], [xraw[c]])

        def convX(c):
            pc = mmbank(4)
            pe([mm(pc.t[:], cdg3[:, c * 4 + k, :], xraw3[:, c, k:k + 512], k == 0, k == 3) for k in range(4)],
               [cdg, xraw[c]], [pc])
            act(xact3[:, c, :], pc.t[:], AF.Silu, [pc, colp], [xact[c]], bias=colp.t[:, C_CB + c:C_CB + c + 1])
            cp("pool", xraw3[:, c, 0:3], xraw3[:, c, 512:515], [xraw[c]], [xraw[c]])

        projX(0)
        for c in range(8):
            if c + 1 < 8:
                projX(c + 1)
            yield W
            convX(c)
        yield W

        ck(3, mixt3[:, 0, 0:512], mixt)
        ck(5, xB3[:, 0, 0:512], xB)
        if seq_start:
            op("pool", lambda e: e.memset(state.t[:], 0.0), [], [state])
            op("pool", lambda e: e.memset(stbf.t[:], 0.0), [], [stbf])

        W = 2.0
        pa = mmbank(4)
        fns = []
        for i in range(4):
            da_i = dav.t[:, i * 8:(i + 1) * 8]
            fns.append(mm(pa.t[:, i * 16:i * 16 + 8], trif.t[:], da_i, True, True))
            fns.append(mm(pa.t[:, i * 16 + 8:i * 16 + 16], onesf.t[:], da_i, True, True))
        pe(fns, [trif, onesf, dav], [pa])
        ea_in, ea_out = ein4, eout4
        cp("dve", v3(ea_in.t[:], 4)[:, :, 0:16], v3(pa.t[:, 0:64], 4), [pa], [ea_in])
        tt("dve", v3(ea_in.t[:], 4)[:, :, 16:24], v3(ea_in.t[:], 4)[:, :, 8:16], v3(ea_in.t[:], 4)[:, :, 0:8], ALU.subtract, [ea_in], [ea_in])
        act(v3(ea_out.t[:], 4)[:, :, 0:24], v3(ea_in.t[:], 4)[:, :, 0:24], AF.Exp, [ea_in], [ea_out])
        tt("dve", v3(ea_out.t[:], 4)[:, :, 24:32], v3(ea_out.t[:], 4)[:, :, 16:24], v3(dtv.t[:], 4), ALU.mult, [ea_out, dtv], [ea_out])
        yield W

        def ssd_front(i):
            tsl = slice(i * 128, (i + 1) * 128)
            R_, d_ = Rb2[i % 2], dec2[i % 2]
            for h in range(8):
                ts("pool", R_.t[:, h * 128:(h + 1) * 128], trib.t[:], dav.t[:, i * 8 + h:i * 8 + h + 1], ALU.mult, [trib, dav], [R_])
            pe([mm(M[2].t[:, gg * 128:(gg + 1) * 128], xact3[:, 4 + gg, tsl], xact3[:, 6 + gg, tsl], True, True) for gg in range(2)],
               xact[4:8], [M[2]])
            tt("dve", v3(cbm.t[:], 2), v3(M[2].t[:, 0:256], 2), trib.t[:].unsqueeze(1).to_broadcast([128, 2, 128]), ALU.mult,
               [M[2], trib], [cbm])
            pe([mm(M[0].t[:], m1b.t[:], R_.t[:, 0:512], True, True)], [m1b, R_], [M[0]])
            pe([mm(M[1].t[:], m1b.t[:], R_.t[:, 512:1024], True, True)], [m1b, R_], [M[1]])
            act(d_.t[:, 0:512], M[0].t[:], AF.Exp, [M[0]], [d_])
            act(d_.t[:, 512:1024], M[1].t[:], AF.Exp, [M[1]], [d_])

        def ssd_mid(i):
            R_, d_ = Rb2[i % 2], dec2[i % 2]
            for h in range(8):
                stt(R_.t[:, h * 128:(h + 1) * 128], d_.t[:, h * 128:(h + 1) * 128], dtv.t[:, i * 8 + h:i * 8 + h + 1],
                    cbm.t[:, (h // 4) * 128:(h // 4 + 1) * 128], ALU.mult, ALU.mult, [d_, dtv, cbm], [R_])
            tt("dve", v3(xw.t[:], 8), v3(xB3[:, i, 0:512], 8), ea_out.t[:, i * 32 + 24:i * 32 + 32].unsqueeze(2).to_broadcast([128, 8, 64]), ALU.mult,
               [xB[i], ea_out], [xw])

        def ssd_back(i):
            tsl = slice(i * 128, (i + 1) * 128)
            R_ = Rb2[i % 2]
            eo0 = i * 32
            fns = []
            for h in range(8):
                fns.append(mm(M[3].t[:, h * 64:(h + 1) * 64], R_.t[:, h * 128:(h + 1) * 128], xB3[:, i, h * 64:(h + 1) * 64], True, False))
                fns.append(mm(M[3].t[:, h * 64:(h + 1) * 64], dsk3[:, h, :], xB3[:, i, h * 64:(h + 1) * 64], False, True))
            pe(fns, [R_, xB[i], dsk], [M[3]])
            yo = M[0]
            pe([mm(yo.t[:, gg * 256:(gg + 1) * 256], xact3[:, 6 + gg, tsl], stbf.t[:, gg * 256:(gg + 1) * 256], True, True) for gg in range(2)],
               xact[6:8] + [stbf], [yo])
            sp_ = M[1]
            pe([mm(sp_.t[:, gg * 256:(gg + 1) * 256], xB3[:, i, 512 + gg * 128:512 + (gg + 1) * 128], xw.t[:, gg * 256:(gg + 1) * 256], True, True)
                for gg in range(2)], [xB[i], xw], [sp_])
            tt("dve", v3(state.t[:], 8), v3(state.t[:], 8), ea_out.t[:, eo0 + 8:eo0 + 16].unsqueeze(2).to_broadcast([128, 8, 64]), ALU.mult, [state, ea_out], [state])
            tt("dve", state.t[:], state.t[:], sp_.t[:], ALU.add, [state, sp_], [state])
            cp("act", stbf.t[:], state.t[:], [state], [stbf])
            tt("dve", v3(t1.t[:], 8), v3(yo.t[:], 8), ea_out.t[:, eo0:eo0 + 8].unsqueeze(2).to_broadcast([128, 8, 64]), ALU.mult, [yo, ea_out], [t1])
            tt("dve", t1.t[:], M[3].t[:], t1.t[:], ALU.add, [M[3], t1], [t1])
            tt("dve", t1.t[:], t1.t[:], mixt3[:, i, 512:1024], ALU.mult, [t1, mixt[i]], [t1])
            sq_ = ssq[i % 2]
            for gg in range(2):
                act(junk.t[:, 0:256], t1.t[:, gg * 256:(gg + 1) * 256], AF.Square, [t1], [junk, sq_], accum=sq_.t[:, gg:gg + 1])
            rsqrt_small(rg[i % 2].t[:, 0:2], sq_.t[:, 0:2], sq_.t[:, 2:4], 1.0 / 256, 2, sq_, sq_, rg[i % 2])
            for gg in range(2):
                stt(mixt3[:, i, 512 + gg * 256:512 + (gg + 1) * 256], t1.t[:, gg * 256:(gg + 1) * 256], rg[i % 2].t[:, gg:gg + 1],
                    rowb.t[:, R_SSMW + gg * 256:R_SSMW + (gg + 1) * 256], ALU.mult, ALU.mult, [t1, rg[i % 2], rowb], [mixt[i]])

        ssd_front(0)
        yield W
        for i in range(4):
            pm = mmbank(4)
            fns = []
            for h in range(8):
                fns.append(mm(pm.t[:, h * 64:(h + 1) * 64], wsT3[:, h, :], vl[i].t[:, h * 64:(h + 1) * 64], True, False))
                fns.append(mm(pm.t[:, h * 64:(h + 1) * 64], bsb_t[0:1, h * 128:(h + 1) * 128], onesr_t[0:1, 0:64], False, True))
            pe(fns, [vl[i], wsT, bsb, onesr], [pm])
            tt("dve", mixt3[:, i, 0:512], pm.t[:], mixt3[:, i, 0:512], ALU.mult, [pm, mixt[i]], [mixt[i]])
            tb = tbank()
            pe([tr(tb.t[:, c * 128:(c + 1) * 128], xact3[:, c, i * 128:(i + 1) * 128]) for c in range(6)],
               xact[0:6] + [ident], [tb])
            cp("act", xB3[:, i, :], tb.t[:, 0:768], [tb], [xB[i]])
            yield W
        for i in range(4):
            ssd_mid(i)
            yield W
            if i + 1 < 4:
                ssd_front(i + 1)
                yield W
            ssd_back(i)
            yield W
        ck(6, mixt3[:, 0, 512:1024], mixt)

        s_o0 = load_slot(5)
        s_o1 = load_slot(6)
        so3 = [v3(s_o0.t[:], 8), v3(s_o1.t[:], 8)]
        so_t = [s_o0, s_o1]

        def f_proj(i):
            tb = tbank()
            pe([tr(tb.t[:, k * 128:(k + 1) * 128], mixt3[:, i, k * 128:(k + 1) * 128]) for k in range(8)], [mixt[i], ident], [tb])
            cp("act", hT3[:, :, i * 128:(i + 1) * 128], v3(tb.t[:], 8), [tb], [hT[i]])
            ob = [M[0], M[1]] if i % 2 == 0 else [M[2], M[3]]
            so_ = ssO[i]
            for n in range(2):
                pe([mm(ob[n].t[:], hT3[:, k, i * 128:(i + 1) * 128], so3[n][:, k, :], k == 0, k == 7) for k in range(8)],
                   [hT[i], so_t[n]], [ob[n]])
                act(junk.t[:, 0:512], ob[n].t[:], AF.Square, [ob[n]], [junk, so_], accum=so_.t[:, n:n + 1])

        def f_post(i):
            ob = [M[0], M[1]] if i % 2 == 0 else [M[2], M[3]]
            so_ = ssO[i]
            tt("dve", so_.t[:, 2:3], so_.t[:, 0:1], so_.t[:, 1:2], ALU.add, [so_], [so_])
            rsqrt_small(rsO[i].t[:, 0:1], so_.t[:, 2:3], so_.t[:, 3:4], 1.0 / 1024, 1, so_, so_, rsO[i])
            for n in range(2):
                t_ = tt_[n]
                stt(t_.t[:], ob[n].t[:], rsO[i].t[:, 0:1], rowb.t[:, R_POST + n * 512:R_POST + (n + 1) * 512], ALU.mult, ALU.mult,
                    [ob[n], rsO[i], rowb], [t_])
                tt("dve", xs[i].t[:, n * 512:(n + 1) * 512], xs[i].t[:, n * 512:(n + 1) * 512], t_.t[:], ALU.add, [xs[i], t_], [xs[i]])
            r0 = tok0 + i * 128
            dma("sp", out_d[r0:r0 + 128, :], xs[i].t[:], [xs[i]], [outT[i]])
            act(junk.t[:], xs[i].t[:], AF.Square, [xs[i]], [junk, ssF[i]], accum=ssF[i].t[:, 0:1])
            rsqrt_small(rsF[i].t[:, 0:1], ssF[i].t[:, 0:1], ssF[i].t[:, 1:2], 1.0 / 1024, 1, ssF[i], ssF[i], rsF[i])
            ts("dve", hb[i % 2].t[:], xs[i].t[:], rsF[i].t[:, 0:1], ALU.mult, [xs[i], rsF[i]], [hb[i % 2]])

        def f_tr(i):
            h_ = hb[i % 2]
            tb = tbank()
            pe([tr(tb.t[:, k * 128:(k + 1) * 128], h_.t[:, k * 128:(k + 1) * 128]) for k in range(8)], [h_, ident], [tb])
            tt("dve", h2T3[:, :, i * 128:(i + 1) * 128], v3(tb.t[:], 8), wpre_b(C_WFFN), ALU.mult, [tb, colp], [h2T[i]])

        f_proj(0)
        yield W
        for i in range(4):
            if i + 1 < 4:
                f_proj(i + 1)
                if i + 1 == 3 and g + 1 < NG and stop == 0:
                    preload[(g + 1, 1)] = load_slot(1)
                    preload[(g + 1, 2)] = load_slot(2)
                yield W
            f_post(i)
            yield W
            if i >= 1:
                if i == 1:
                    yield "drain"
                f_tr(i - 1)
                yield W
        f_tr(3)
        if g + 1 < NG and stop == 0:
            prodA(0, tok0 + 512)
            prodA(1, tok0 + 512)
            prodA1(2, tok0 + 512)
            prodA1(3, tok0 + 512)
            pre_issued[g + 1] = True
        yield W
        ck(7, h2T3[:, 0, :], h2T)
        ffn_args[g] = outT

    ffn_args = {}
    frr = {"k": 0}

    def fbank():
        b = frr["k"] % 2
        frr["k"] += 1
        return Fb[b]

    def ffn(g):
        tok0 = g * 512
        outT = ffn_args[g]
        for j in range(8):
            slot = load_slot(7 + j, "f")
            sl3 = v3(slot.t[:], 8)
            for cc in range(4):
                f = j * 4 + cc
                ps_ = fbank()
                pe([mm(ps_.t[:], sl3[:, k, cc * 128:(cc + 1) * 128], h2T3[:, k, :], k == 0, k == 7) for k in range(8)],
                   h2T + [slot], [ps_])
                r_ = rl[f % 2]
                act(r_.t[:], ps_.t[:], AF.Relu, [ps_], [r_])
                tt("dve", fT3[:, f, :], r_.t[:], r_.t[:], ALU.mult, [r_], [fT[f]])
                yield
        ck(8, fT3[:, 0, :], fT)
        o2T3 = h2T3
        for dp in range(4):
            for qh in range(2):
                slot = load_slot(DOWN_BASE + dp * 2 + qh, "f")
                sl3 = v3(slot.t[:], 16)
                for db in range(2):
                    bank = Fb[db]
                    pe([mm(bank.t[:], sl3[:, kk, db * 128:(db + 1) * 128], fT3[:, qh * 16 + kk, :],
                           (qh == 0 and kk == 0), (qh == 1 and kk == 15)) for kk in range(16)],
                       fT[qh * 16:(qh + 1) * 16] + [slot], [bank])
                    if qh == 1:
                        act(o2T3[:, dp * 2 + db, :], bank.t[:], AF.Copy, [bank], h2T)
                    yield
        for i in range(4):
            tb = tbank()
            pe([tr(tb.t[:, k * 128:(k + 1) * 128], o2T3[:, k, i * 128:(i + 1) * 128]) for k in range(8)], [h2T[i], ident], [tb])
            sd_ = ssD[i]
            cp("dve", o2k.t[:], tb.t[:], [tb], [o2k])
            act(junk.t[:], o2k.t[:], AF.Square, [o2k], [junk, sd_], accum=sd_.t[:, 0:1])
            rsqrt_small(rsD[i].t[:, 0:1], sd_.t[:, 0:1], sd_.t[:, 1:2], 1.0 / 1024, 1, sd_, sd_, rsD[i])
            r0 = tok0 + i * 128
            for nn in range(2):
                t_ = ttF[nn]
                stt(t_.t[:], o2k.t[:, nn * 512:(nn + 1) * 512], rsD[i].t[:, 0:1], rowb.t[:, R_POST2 + nn * 512:R_POST2 + (nn + 1) * 512],
                    ALU.mult, ALU.mult, [o2k, rsD[i], rowb], [t_])
                dma("pool", out_d[r0:r0 + 128, nn * 512:(nn + 1) * 512], t_.t[:], [t_, outT[i]], [outT[i]], accum=True)
            yield

    def run_pair(a, b):
        budget = 0.0
        a_live, b_live = a is not None, b is not None
        while a_live or b_live:
            if a_live:
                try:
                    r = next(a)
                except StopIteration:
                    a_live = False
                    r = None
                convert_some(1)
                if r == "drain":
                    while b_live:
                        try:
                            next(b)
                        except StopIteration:
                            b_live = False
                elif isinstance(r, float):
                    budget += r
            else:
                budget += 1.0
            while b_live and budget >= 1.0:
                budget -= 1.0
                try:
                    next(b)
                except StopIteration:
                    b_live = False
            if not b_live:
                budget = 0.0

    try:
        prev = None
        for g in range(NG):
            run_pair(mixer(g), prev)
            prev = ffn(g)
        run_pair(None, prev)
    except _Stop:
        pass
    P.finish()
    P.emit()
    return nc


def _pack_weights(w_in, w_out, w_up, w_down):
    def slot(w, c0):
        blk = w[:, c0:c0 + 512].reshape(8, 128, 512).transpose(1, 0, 2)
        return np.ascontiguousarray(blk).reshape(128, 4096)

    sl = []
    for s in range(5):
        sl.append(slot(w_in, s * 512))
    for n in range(2):
        sl.append(slot(w_out, n * 512))
    for j in range(8):
        sl.append(slot(w_up, j * 512))
    for dp in range(4):
        for qh in range(2):
            blk = w_down[qh * 2048:(qh + 1) * 2048, dp * 256:(dp + 1) * 256].reshape(16, 128, 256).transpose(1, 0, 2)
            sl.append(np.ascontiguousarray(blk).reshape(128, 4096))
    return np.stack(sl, 0).astype(np.float32)


def make_in_maps(inputs, ncores, NG):
    x = np.asarray(inputs["x"], np.float32)
    w_in = np.asarray(inputs["w_in"], np.float32)[0]
    wsl = _pack_weights(w_in, np.asarray(inputs["w_out"], np.float32)[0], np.asarray(inputs["w_up"], np.float32)[0],
                        np.asarray(inputs["w_down"], np.float32)[0])
    wdt = np.ascontiguousarray(w_in[:, 2560:2568].reshape(8, 128, 8).transpose(1, 0, 2)).reshape(128, 64)
    g = lambda k: np.asarray(inputs[k], np.float32)[0]
    rowp = np.concatenate([g("norm_mix_post"), g("norm_ffn_post"), g("gm_ln_w").reshape(-1), g("gm_ln_b").reshape(-1),
                           g("ssm_norm_w"), g("dt_bias"), g("a_log"), g("d_skip")]).astype(np.float32)
    assert rowp.shape[0] == R_END
    colp = np.zeros((128, C_END), np.float32)
    colp[:, C_WPRE:C_WPRE + 8] = g("norm_mix_pre").reshape(8, 128).T
    colp[:, C_WFFN:C_WFFN + 8] = g("norm_ffn_pre").reshape(8, 128).T
    cw = g("conv_w")
    colp[:, C_CW:C_CW + 32] = cw.reshape(4, 8, 128).transpose(2, 1, 0).reshape(128, 32)
    colp[:, C_CB:C_CB + 8] = g("conv_b").reshape(8, 128).T
    ws = g("gm_w_s")
    wst = np.ascontiguousarray(ws.transpose(2, 0, 1)).reshape(128, 1024)
    bs = g("gm_b_s").reshape(1, 1024)
    xf = x.reshape(-1, 1024)
    ntok = NG * 512
    maps = []
    for c in range(ncores):
        maps.append({"x": np.ascontiguousarray(xf[c * ntok:(c + 1) * ntok]), "wslots": wsl, "wdt": wdt.astype(np.float32),
                     "rowpack": rowp, "colpack": colp, "wst": wst.astype(np.float32), "bsrow": bs.astype(np.float32)})
    return maps


def kernel(**inputs):
    NG = 8
    ncores = 8
    nc = build(NG)
    maps = make_in_maps(inputs, ncores, NG)
    res = run_bass_kernel_spmd(nc, maps, core_ids=list(range(ncores)))
    outs = [np.asarray(r["out"], np.float32) for r in res.results]
    return np.concatenate(outs, 0).reshape(16, 2048, 1024)
```

```python
from contextlib import ExitStack
import numpy as np
import os
XV = os.environ.get('XV', '')
import concourse.bass as bass
import concourse.mybir as mybir
from concourse.bass_utils import run_bass_kernel_spmd

F32 = mybir.dt.float32
BF16 = mybir.dt.bfloat16
AF = mybir.ActivationFunctionType
ALU = mybir.AluOpType
AX = mybir.AxisListType

EPS = 1e-6
NSLOT = 23
RING = 4
DOWN_BASE = 15
R_POST, R_POST2, R_LNW, R_LNB, R_SSMW, R_DTB, R_ALOG, R_DSK, R_END = 0, 1024, 2048, 2560, 3072, 3584, 3592, 3600, 3608
C_WPRE, C_WFFN, C_CW, C_CB, C_END = 0, 8, 16, 48, 56


class Tile:
    def __init__(self, t, name):
        self.t = t
        self.name = name
        self.w = None
        self.r = {}
        self.const = False
        self.psum = False


class Eng:
    def __init__(self, name, sem):
        self.name = name
        self.sem = sem
        self.cnt = 0
        self.seen = {}
        self.prog = []
        self.dsems = []
        self.dval = []
        self.dk = 0


class Planner:
    def __init__(self, nc, es):
        self.nc = nc
        self.es = es
        self.sems = {}
        self.E = {}
        for n in ("pe", "act", "dve", "pool", "sp"):
            s = es.enter_context(nc.semaphore("sem_" + n))
            self.sems[n] = s
            self.E[n] = Eng(n, s)
        for q, k in (("sp", 12), ("pool", 26)):
            for j in range(k):
                key = "d_%s_%d" % (q, j)
                self.sems[key] = es.enter_context(nc.semaphore(key))
                self.E[q].dsems.append(key)
                self.E[q].dval.append(0)
        self.nsb = 0

    def sb(self, name, cols, dt):
        t = self.es.enter_context(self.nc.sbuf_tensor("s_" + name, [128, cols], dt))
        return Tile(t, name)

    def sb_multi(self, name, cols, dt, nparts):
        t = self.es.enter_context(self.nc.sbuf_tensor("s_" + name, [128, cols], dt))
        return t, [Tile(t, "%s_%d" % (name, i)) for i in range(nparts)]

    def ps(self, name, cols, dt):
        t = self.es.enter_context(self.nc.psum_tensor("p_" + name, [128, cols], dt))
        tl = Tile(t, name)
        tl.psum = True
        return tl

    def _deps(self, E, reads, writes):
        deps = {}

        def add(tk):
            if tk is None:
                return
            k, v = tk
            if deps.get(k, 0) < v:
                deps[k] = v

        for t in reads:
            add(t.w)
            if t.psum:
                for k, v in t.r.items():
                    if k != E.name:
                        add((k, v))
        for t in writes:
            add(t.w)
            for k, v in t.r.items():
                add((k, v))
        for k, v in deps.items():
            if k == E.name and E.name == "pe":
                continue
            if E.seen.get(k, 0) >= v:
                continue
            E.seen[k] = v
            sem = self.sems[k]
            E.prog.append(lambda e, sem=sem, v=v: e.wait_ge(sem, v))

    def _record(self, tk, reads, writes):
        k, v = tk
        for t in reads:
            if not t.const:
                if t.r.get(k, 0) < v:
                    t.r[k] = v
        for t in writes:
            t.w = tk
            t.r = {}

    def op(self, eng, fn, reads=(), writes=()):
        E = self.E[eng]
        self._deps(E, reads, writes)
        E.cnt += 1
        sem = E.sem
        E.prog.append(lambda e, fn=fn, sem=sem: fn(e).then_inc(sem, 1))
        self._record((E.name, E.cnt), reads, writes)

    def pe(self, fns, reads=(), writes=(), inc=True):
        E = self.E["pe"]
        self._deps(E, reads, writes)
        if not inc:
            for fn in fns:
                E.prog.append(lambda e, fn=fn: fn(e))
            self._record((E.name, E.cnt + 1), reads, writes)
            return
        for fn in fns[:-1]:
            E.prog.append(lambda e, fn=fn: fn(e))
        E.cnt += 1
        sem = E.sem
        E.prog.append(lambda e, fn=fns[-1], sem=sem: fn(e).then_inc(sem, 1))
        self._record((E.name, E.cnt), reads, writes)

    def dma(self, q, out, in_, reads=(), writes=(), accum=False):
        E = self.E[q]
        self._deps(E, reads, writes)
        k = E.dk
        E.dk = (E.dk + 1) % len(E.dsems)
        key = E.dsems[k]
        sem = self.sems[key]
        prev = E.dval[k]
        if prev > 0 and E.seen.get(key, 0) < prev:
            E.seen[key] = prev
            E.prog.append(lambda e, sem=sem, v=prev: e.wait_ge(sem, v))
        E.dval[k] += 16
        if accum:
            E.prog.append(lambda e, out=out, in_=in_, sem=sem: e.dma_start(out=out, in_=in_, accum_op=ALU.add).then_inc(sem, 16))
        else:
            E.prog.append(lambda e, out=out, in_=in_, sem=sem: e.dma_start(out=out, in_=in_).then_inc(sem, 16))
        self._record((key, E.dval[k]), reads, writes)

    def finish(self):
        E = self.E["sp"]
        for q in ("sp", "pool"):
            Q = self.E[q]
            for key, v in zip(Q.dsems, Q.dval):
                if v > 0:
                    sem = self.sems[key]
                    E.prog.append(lambda e, sem=sem, v=v: e.wait_ge(sem, v))
        for n in ("pe", "act", "dve", "pool"):
            v = self.E[n].cnt
            if v > 0:
                sem = self.sems[n]
                E.prog.append(lambda e, sem=sem, v=v: e.wait_ge(sem, v))

    def emit(self):
        nc = self.nc
        E = self.E
        with nc.Block() as block:
            @block.sync
            def _(e):
                for f in E["sp"].prog:
                    f(e)

            @block.tensor
            def _(e):
                for f in E["pe"].prog:
                    f(e)

            @block.scalar
            def _(e):
                for f in E["act"].prog:
                    f(e)

            @block.vector
            def _(e):
                for f in E["dve"].prog:
                    f(e)

            @block.gpsimd
            def _(e):
                for f in E["pool"].prog:
                    f(e)


def v3(ap, a):
    return ap.rearrange("p (a b) -> p a b", a=a)


class _Stop(Exception):
    pass


def build(NG=8, stop=0):
    nc = bass.Bass("TRN2", target_bir_lowering=False)
    es = ExitStack()
    NT = NG * 512
    x_d = nc.dram_tensor("x", [NT, 1024], F32, kind="ExternalInput").ap()
    out_d = nc.dram_tensor("out", [NT, 1024], F32, kind="ExternalOutput").ap()
    wsl_d = nc.dram_tensor("wslots", [NSLOT, 128, 4096], F32, kind="ExternalInput").ap()
    wdt_d = nc.dram_tensor("wdt", [128, 64], F32, kind="ExternalInput").ap()
    rowp_d = nc.dram_tensor("rowpack", [R_END], F32, kind="ExternalInput").ap()
    colp_d = nc.dram_tensor("colpack", [128, C_END], F32, kind="ExternalInput").ap()
    wst_d = nc.dram_tensor("wst", [128, 1024], F32, kind="ExternalInput").ap()
    bs_d = nc.dram_tensor("bsrow", [1, 1024], F32, kind="ExternalInput").ap()
    scr_d = nc.dram_tensor("scr", [NSLOT, 128, 4096], BF16, kind="Internal").ap()

    P = Planner(nc, es)
    op, pe, dma = P.op, P.pe, P.dma

    ssD = [P.sb("ssD%d" % i, 4, F32) for i in range(4)]
    rsD = [P.sb("rsD%d" % i, 2, F32) for i in range(4)]
    fT_t, fT = P.sb_multi("fT", 32 * 512, BF16, 32)
    rowb = P.sb("rowb", R_END, F32)
    colp = P.sb("colp", C_END, F32)
    ident = P.sb("ident", 128, BF16)
    onesf = P.sb("onesf", 128, F32)
    trif = P.sb("trif", 128, F32)
    trib = P.sb("trib", 128, BF16)
    m1b = P.sb("m1b", 128, BF16)
    wsT = P.sb("wsT", 1024, BF16)
    dsk = P.sb("dsk", 1024, BF16)
    bsf = es.enter_context(nc.sbuf_tensor("s_bsf", [1, 1024], F32))
    bsb_t = es.enter_context(nc.sbuf_tensor("s_bsb", [1, 1024], BF16))
    bsb = Tile(bsb_t, "bsb")
    bsfT = Tile(bsf, "bsf")
    onesr_t = es.enter_context(nc.sbuf_tensor("s_onesr", [1, 64], BF16))
    onesr = Tile(onesr_t, "onesr")
    nh = P.sb("nh", 8, F32)
    ab = P.sb("ab", 8, F32)
    wdt = P.sb("wdt", 64, BF16)

    ring = [P.sb("ring%d" % r, 4096, BF16) for r in range(RING)]
    xs = [P.sb("xs%d" % i, 1024, F32) for i in range(4)]
    hb = [P.sb("hb%d" % i, 1024, BF16) for i in range(2)]
    junk = P.sb("junk", 1024, BF16)
    hT_t, hT = P.sb_multi("hT", 4096, BF16, 4)
    h2T_t, h2T = P.sb_multi("h2T", 4096, BF16, 4)
    vg = [P.sb("vg0", 512, F32)] * 2
    sq = P.sb("sq", 512, F32)
    vn = P.sb("vn", 512, F32)
    vl = [P.sb("vl%d" % i, 512, BF16) for i in range(4)]
    xraw_t, xraw = P.sb_multi("xraw", 8 * 516, BF16, 8)
    cdg = P.sb("cdg", 32 * 128, BF16)
    xact_t, xact = P.sb_multi("xact", 4096, BF16, 8)
    xB_t, xB = P.sb_multi("xB", 4 * 768, BF16, 4)
    mixt_t, mixt = P.sb_multi("mixt", 4096, BF16, 4)
    Rb2 = [P.sb("Rb%d" % i, 1024, BF16) for i in range(2)]
    dec2 = [P.sb("dec%d" % i, 1024, BF16) for i in range(1)] * 2
    cbm = P.sb("cbm", 256, BF16)
    t1 = P.sb("t1", 512, F32)
    xw = P.sb("xw", 512, BF16)
    state = P.sb("state", 512, F32)
    stbf = P.sb("stbf", 512, BF16)
    tt_ = [P.sb("tt%d" % i, 512, F32) for i in range(2)]
    rl = [P.sb("rl%d" % i, 512, F32) for i in range(2)]
    ttF = [P.sb("ttF%d" % i, 512, F32) for i in range(2)]
    o2k = P.sb("o2k", 1024, BF16)

    def smalls(name, n, cnt):
        return [P.sb("%s%d" % (name, i), n, F32) for i in range(cnt)]

    ssA = smalls("ssA", 2, 4)
    rsA = smalls("rsA", 2, 4)
    lnS = smalls("lnS", 48, 2)
    dtr = P.sb("dtr", 32, F32)
    dta = P.sb("dta", 32, F32)
    dte = P.sb("dte", 32, F32)
    dtv = P.sb("dtv", 32, F32)
    dav = P.sb("dav", 32, F32)
    ein4 = P.sb("ein4", 128, F32)
    eout4 = P.sb("eout4", 128, F32)
    ssq = smalls("ssq", 4, 2)
    rg = smalls("rg", 2, 2)
    ssO = smalls("ssO", 4, 4)
    rsO = smalls("rsO", 2, 4)
    ssF = smalls("ssF", 2, 4)
    rsF = smalls("rsF", 2, 4)

    pb = [P.ps("pb%d" % i, 512, F32) for i in range(6)]
    ptb = [P.ps("ptb%d" % i, 1024, BF16) for i in range(2)]

    scrT = [Tile(None, "scr%d" % s) for s in range(NSLOT)]

    def A(tile):
        return tile.t[:]

    def act(out, in_, func, reads, writes, bias=None, scale=None, accum=None):
        kw = {}
        if bias is not None:
            kw["bias"] = bias
        if scale is not None:
            kw["scale"] = scale
        if accum is not None:
            kw["accum_out"] = accum
        op("act", lambda e: e.activation(out=out, in_=in_, func=func, **kw), reads, writes)

    def ts(eng, out, in0, s1, op0, reads, writes, s2=None, op1=None):
        if op1 is None and eng == "pool":
            op(eng, lambda e: e.tensor_scalar(out=out, in0=in0, scalar1=s1, scalar2=1.0, op0=op0, op1=ALU.mult), reads, writes)
        elif op1 is None:
            op(eng, lambda e: e.tensor_scalar(out=out, in0=in0, scalar1=s1, scalar2=None, op0=op0), reads, writes)
        else:
            op(eng, lambda e: e.tensor_scalar(out=out, in0=in0, scalar1=s1, scalar2=s2, op0=op0, op1=op1), reads, writes)

    def tt(eng, out, in0, in1, o, reads, writes):
        op(eng, lambda e: e.tensor_tensor(out=out, in0=in0, in1=in1, op=o), reads, writes)

    def stt(out, in0, scalar, in1, op0, op1, reads, writes):
        op("dve", lambda e: e.scalar_tensor_tensor(out=out, in0=in0, scalar=scalar, in1=in1, op0=op0, op1=op1), reads, writes)

    def cp(eng, out, in_, reads, writes):
        if eng == "act":
            op(eng, lambda e: e.activation(out=out, in_=in_, func=AF.Copy), reads, writes)
        else:
            op(eng, lambda e: e.tensor_copy(out=out, in_=in_), reads, writes)

    def mm(out, lhsT, rhs, start, stop):
        return lambda e: e.matmul(out=out, lhsT=lhsT, rhs=rhs, start=start, stop=stop)

    def tr(out, in_):
        idn = ident.t[:]
        return lambda e: e.transpose(out=out, in_=in_, identity=idn)

    def rsqrt_small(dst, src, tmp, scale, n, reads_t, tmp_t, dst_t):
        ts("pool", tmp, src, scale, ALU.mult, [reads_t], [tmp_t], s2=EPS, op1=ALU.add)
        tt("pool", dst, tmp, nh.t[:, 0:n], ALU.pow, [tmp_t, nh], [dst_t])

    ring_state = {"m": 0, "f": 0}

    def load_slot(s, pool="m"):
        k = ring_state[pool]
        ring_state[pool] = k + 1
        r = (k % 2) + (0 if pool == "m" else 2)
        assert scrT[s].w is not None, "weight slot %d used before its conversion was issued" % s
        dma("sp", ring[r].t[:], scr_d[s], reads=[scrT[s]], writes=[ring[r]])
        return ring[r]

    def ck(n, src_ap=None, reads=()):
        if stop != n:
            return
        if src_ap is not None:
            ncol = src_ap.shape[-1]
            cp("dve", vn.t[:, 0:ncol], src_ap, list(reads), [vn])
            dma("sp", out_d[0:128, 0:ncol], vn.t[:, 0:ncol], [vn], [])
        raise _Stop()

    dma("sp", rowb.t[:], rowp_d.partition_broadcast(128), [], [rowb])
    dma("sp", colp.t[:], colp_d, [], [colp])
    dma("sp", xs[0].t[:], wst_d, [], [xs[0]])
    dma("sp", bsf[:], bs_d, [], [bsfT])
    dma("pool", wdt.t[:], wdt_d, [], [wdt])
    op("pool", lambda e: e.memset(onesf.t[:], 1.0), [], [onesf])
    op("pool", lambda e: e.memset(nh.t[:], -0.5), [], [nh])
    op("pool", lambda e: e.memset(onesr_t[:], 1.0), [], [onesr])
    op("pool", lambda e: e.affine_select(out=ident.t[:], in_=onesf.t[:], pattern=[[-1, 128]], compare_op=ALU.is_equal,
                                          fill=0.0, base=0, channel_multiplier=1), [onesf], [ident])
    op("pool", lambda e: e.affine_select(out=trif.t[:], in_=onesf.t[:], pattern=[[1, 128]], compare_op=ALU.is_ge,
                                          fill=0.0, base=0, channel_multiplier=-1), [onesf], [trif])
    op("pool", lambda e: e.affine_select(out=trib.t[:], in_=onesf.t[:], pattern=[[1, 128]], compare_op=ALU.is_ge,
                                          fill=0.0, base=0, channel_multiplier=-1), [onesf], [trib])
    op("pool", lambda e: e.affine_select(out=m1b.t[:], in_=onesf.t[:], pattern=[[-1, 128]], compare_op=ALU.is_gt,
                                          fill=0.0, base=0, channel_multiplier=1), [onesf], [m1b])
    conv_order = [1, 2, 0, 3, 4, 5, 6] + list(range(7, NSLOT))
    conv_state = {"k": 0}

    def convert_some(n):
        for _ in range(n):
            k = conv_state["k"]
            if k >= NSLOT:
                return
            sl_ = conv_order[k]
            conv_state["k"] = k + 1
            dma("pool", scr_d[sl_], wsl_d[sl_], [], [scrT[sl_]])

    convert_some(5)
    tt("dve", v3(wsT.t[:], 8), v3(xs[0].t[:], 8), trif.t[:].unsqueeze(1).to_broadcast([128, 8, 128]), ALU.mult,
       [xs[0], trif], [wsT])
    cp("dve", bsb_t[:], bsf[:], [bsfT], [bsb])
    act(ab.t[:], rowb.t[:, R_ALOG:R_ALOG + 8], AF.Exp, [rowb], [ab])
    ts("dve", ab.t[:], ab.t[:], -1.0, ALU.mult, [ab], [ab])
    for h in range(8):
        ts("dve", dsk.t[:, h * 128:(h + 1) * 128], ident.t[:], rowb.t[:, R_DSK + h:R_DSK + h + 1], ALU.mult,
           [ident, rowb], [dsk])
    for j in range(32):
        ts("dve", cdg.t[:, j * 128:(j + 1) * 128], ident.t[:], colp.t[:, C_CW + j:C_CW + j + 1], ALU.mult, [ident, colp], [cdg])
    for t_ in (rowb, colp, ident, onesf, trif, trib, m1b, wsT, dsk, bsb, onesr, nh, ab, wdt, cdg):
        t_.const = True
    for t_ in scrT:
        t_.const = True

    try:
        ck(1, wsT.t[:, 0:512], [wsT])
    except _Stop:
        P.finish(); P.emit(); return nc
    hT3 = v3(hT_t[:], 8)
    h2T3 = v3(h2T_t[:], 8)
    xraw3 = v3(xraw_t[:], 8)
    cdg3 = v3(cdg.t[:], 32)
    xact3 = v3(xact_t[:], 8)
    xB3 = v3(xB_t[:], 4)
    mixt3 = v3(mixt_t[:], 4)
    fT3 = v3(fT_t[:], 32)
    wsT3 = v3(wsT.t[:], 8)
    dsk3 = v3(dsk.t[:], 8)
    wdt3 = v3(wdt.t[:], 8)

    def wpre_b(off):
        return colp.t[:, off:off + 8].unsqueeze(2).to_broadcast([128, 8, 128])

    bank_rr = {"k": 0}

    def mmbank(n):
        b = bank_rr["k"] % n
        bank_rr["k"] += 1
        return pb[b]

    tb_rr = {"k": 0}

    def tbank():
        b = tb_rr["k"] % 2
        tb_rr["k"] += 1
        return ptb[b]

    M = pb[0:4]
    Fb = pb[4:6]

    pre_issued = {}
    preload = {}

    def mixer(g):
        tok0 = g * 512
        seq_start = (g % 4 == 0)
        outT = [Tile(None, "out_%d_%d" % (g, i)) for i in range(4)]

        W = 0.5
        def prodA1(i, base=None):
            r0 = (tok0 if base is None else base) + i * 128
            dma("sp", xs[i].t[:], x_d[r0:r0 + 128, :], [], [xs[i]])
            act(junk.t[:], xs[i].t[:], AF.Square, [xs[i]], [junk, ssA[i]], accum=ssA[i].t[:, 0:1])
            rsqrt_small(rsA[i].t[:, 0:1], ssA[i].t[:, 0:1], ssA[i].t[:, 1:2], 1.0 / 1024, 1, ssA[i], ssA[i], rsA[i])

        def prodA(i, base=None, front=True):
            if front:
                prodA1(i, base)
            ts("dve", hb[i % 2].t[:], xs[i].t[:], rsA[i].t[:, 0:1], ALU.mult, [xs[i], rsA[i]], [hb[i % 2]])

        def consA(i):
            h_ = hb[i % 2]
            tb = tbank()
            pe([tr(tb.t[:, k * 128:(k + 1) * 128], h_.t[:, k * 128:(k + 1) * 128]) for k in range(8)], [h_, ident], [tb])
            tt("dve", hT3[:, :, i * 128:(i + 1) * 128], v3(tb.t[:], 8), wpre_b(C_WPRE), ALU.mult, [tb, colp], [hT[i]])

        if not pre_issued.get(g):
            prodA(0)
            prodA(1)
            yield W
            yield W
        for i in range(4):
            consA(i)
            if i + 2 < 4:
                prodA(i + 2, front=not pre_issued.get(g))
            yield W
        ck(2, hT3[:, 0, :], hT)

        pdt = mmbank(4)
        for i in range(4):
            pe([mm(pdt.t[:, i * 8:(i + 1) * 8], hT3[:, k, i * 128:(i + 1) * 128], wdt3[:, k, :], k == 0, k == 7) for k in range(8)],
               [hT[i], wdt], [pdt])
        tt("dve", v3(dtr.t[:], 4), v3(pdt.t[:, 0:32], 4), rowb.t[:, R_DTB:R_DTB + 8].unsqueeze(1).to_broadcast([128, 4, 8]),
           ALU.add, [pdt, rowb], [dtr])
        stt(dta.t[:], dtr.t[:], -1.0, dtr.t[:], ALU.mult, ALU.max, [dtr], [dta])
        act(dte.t[:], dta.t[:], AF.Exp, [dta], [dte], scale=-1.0)
        act(dte.t[:], dte.t[:], AF.Ln, [dte], [dte], bias=1.0)
        stt(dtv.t[:], dtr.t[:], 0.0, dte.t[:], ALU.max, ALU.add, [dtr, dte], [dtv])
        tt("dve", v3(dav.t[:], 4), v3(dtv.t[:], 4), ab.t[:, 0:8].unsqueeze(1).to_broadcast([128, 4, 8]), ALU.mult, [dtv, ab], [dav])
        yield W
        def lnchain(i, ps_):
            v_ = vg[i % 2]
            ln = lnS[i % 2]
            vl_ = vl[i]
            act(v_.t[:], ps_.t[:], AF.Gelu_apprx_tanh, [ps_], [v_])
            op("dve", lambda e, v_=v_, ln=ln: e.tensor_reduce(out=ln.t[:, 0:8], in_=v3(v_.t[:], 8), axis=AX.X, op=ALU.add),
               [v_], [ln])
            act(sq.t[:], v_.t[:], AF.Square, [v_], [sq])
            op("dve", lambda e, ln=ln: e.tensor_reduce(out=ln.t[:, 8:16], in_=v3(sq.t[:], 8), axis=AX.X, op=ALU.add),
               [sq, ln], [ln])
            ts("dve", ln.t[:, 16:24], ln.t[:, 0:8], 1.0 / 64, ALU.mult, [ln], [ln])
            tt("dve", ln.t[:, 24:32], ln.t[:, 16:24], ln.t[:, 16:24], ALU.mult, [ln], [ln])
            stt(ln.t[:, 24:32], ln.t[:, 8:16], 1.0 / 64, ln.t[:, 24:32], ALU.mult, ALU.subtract, [ln], [ln])
            ts("pool", ln.t[:, 24:32], ln.t[:, 24:32], 1.0, ALU.mult, [ln], [ln], s2=EPS, op1=ALU.add)
            tt("pool", ln.t[:, 32:40], ln.t[:, 24:32], nh.t[:, 0:8], ALU.pow, [ln, nh], [ln])
            tt("dve", v3(vn.t[:], 8), v3(v_.t[:], 8), ln.t[:, 16:24].unsqueeze(2).to_broadcast([128, 8, 64]),
               ALU.subtract, [v_, ln], [vn])
            tt("dve", v3(vn.t[:], 8), v3(vn.t[:], 8), ln.t[:, 32:40].unsqueeze(2).to_broadcast([128, 8, 64]),
               ALU.mult, [vn, ln], [vn])
            tt("dve", vn.t[:], vn.t[:], rowb.t[:, R_LNW:R_LNW + 512], ALU.mult, [vn, rowb], [vn])
            tt("dve", vl_.t[:], vn.t[:], rowb.t[:, R_LNB:R_LNB + 512], ALU.add, [vn, rowb], [vl_])

        for blk in (1, 2, 0):
            slot = preload.pop((g, blk), None) or load_slot(blk)
            sl3 = v3(slot.t[:], 8)
            for i in range(4):
                ps_ = mmbank(4)
                pe([mm(ps_.t[:], hT3[:, k, i * 128:(i + 1) * 128], sl3[:, k, :], k == 0, k == 7) for k in range(8)],
                   [hT[i], slot], [ps_])
                if blk == 0:
                    act(mixt3[:, i, 0:512], ps_.t[:], AF.Gelu_apprx_tanh, [ps_], [mixt[i]])
                elif blk == 2:
                    act(mixt3[:, i, 512:1024], ps_.t[:], AF.Silu, [ps_], [mixt[i]])
                else:
                    lnchain(i, ps_)
                yield W
        xslots = {}

        def projX(c):
            sl = 3 + c // 4
            cc = c % 4
            if cc == 0:
                xslots[sl] = load_slot(sl)
            slot = xslots[sl]
            sl3 = v3(slot.t[:], 8)
            ps_ = mmbank(4)
            pe([mm(ps_.t[:], sl3[:, k, cc * 128:(cc + 1) * 128], hT3[:, k, :], k == 0, k == 7) for k in range(8)],
               hT + [slot], [ps_])
            if seq_start:
                op("pool", lambda e, c=c: e.memset(xraw3[:, c, 0:4], 0.0), [], [xraw[c]])
            act(xraw3[:, c, 3:515], ps_.t[:], AF.Copy, [ps_], [xraw[c]])

        def convX(c):
            pc = mmbank(4)
            pe([mm(pc.t[:], cdg3[:, c * 4 + k, :], xraw3[:, c, k:k + 512], k == 0, k == 3) for k in range(4)],
               [cdg, xraw[c]], [pc])
            act(xact3[:, c, :], pc.t[:], AF.Silu, [pc, colp], [xact[c]], bias=colp.t[:, C_CB + c:C_CB + c + 1])
            cp("pool", xraw3[:, c, 0:3], xraw3[:, c, 512:515], [xraw[c]], [xraw[c]])

        projX(0)
        for c in range(8):
            if c + 1 < 8:
                projX(c + 1)
            yield W
            convX(c)
        yield W

        ck(3, mixt3[:, 0, 0:512], mixt)
        ck(5, xB3[:, 0, 0:512], xB)
        if seq_start:
            op("pool", lambda e: e.memset(state.t[:], 0.0), [], [state])
            op("pool", lambda e: e.memset(stbf.t[:], 0.0), [], [stbf])

        W = 2.0
        pa = mmbank(4)
        fns = []
        for i in range(4):
            da_i = dav.t[:, i * 8:(i + 1) * 8]
            fns.append(mm(pa.t[:, i * 16:i * 16 + 8], trif.t[:], da_i, True, True))
            fns.append(mm(pa.t[:, i * 16 + 8:i * 16 + 16], onesf.t[:], da_i, True, True))
        pe(fns, [trif, onesf, dav], [pa])
        ea_in, ea_out = ein4, eout4
        cp("dve", v3(ea_in.t[:], 4)[:, :, 0:16], v3(pa.t[:, 0:64], 4), [pa], [ea_in])
        tt("dve", v3(ea_in.t[:], 4)[:, :, 16:24], v3(ea_in.t[:], 4)[:, :, 8:16], v3(ea_in.t[:], 4)[:, :, 0:8], ALU.subtract, [ea_in], [ea_in])
        act(v3(ea_out.t[:], 4)[:, :, 0:24], v3(ea_in.t[:], 4)[:, :, 0:24], AF.Exp, [ea_in], [ea_out])
        tt("dve", v3(ea_out.t[:], 4)[:, :, 24:32], v3(ea_out.t[:], 4)[:, :, 16:24], v3(dtv.t[:], 4), ALU.mult, [ea_out, dtv], [ea_out])
        yield W

        def ssd_front(i):
            tsl = slice(i * 128, (i + 1) * 128)
            R_, d_ = Rb2[i % 2], dec2[i % 2]
            for h in range(8):
                ts("pool", R_.t[:, h * 128:(h + 1) * 128], trib.t[:], dav.t[:, i * 8 + h:i * 8 + h + 1], ALU.mult, [trib, dav], [R_])
            pe([mm(M[2].t[:, gg * 128:(gg + 1) * 128], xact3[:, 4 + gg, tsl], xact3[:, 6 + gg, tsl], True, True) for gg in range(2)],
               xact[4:8], [M[2]])
            tt("dve", v3(cbm.t[:], 2), v3(M[2].t[:, 0:256], 2), trib.t[:].unsqueeze(1).to_broadcast([128, 2, 128]), ALU.mult,
               [M[2], trib], [cbm])
            pe([mm(M[0].t[:], m1b.t[:], R_.t[:, 0:512], True, True)], [m1b, R_], [M[0]])
            pe([mm(M[1].t[:], m1b.t[:], R_.t[:, 512:1024], True, True)], [m1b, R_], [M[1]])
            act(d_.t[:, 0:512], M[0].t[:], AF.Exp, [M[0]], [d_])
            act(d_.t[:, 512:1024], M[1].t[:], AF.Exp, [M[1]], [d_])

        def ssd_mid(i):
            R_, d_ = Rb2[i % 2], dec2[i % 2]
            for h in range(8):
                stt(R_.t[:, h * 128:(h + 1) * 128], d_.t[:, h * 128:(h + 1) * 128], dtv.t[:, i * 8 + h:i * 8 + h + 1],
                    cbm.t[:, (h // 4) * 128:(h // 4 + 1) * 128], ALU.mult, ALU.mult, [d_, dtv, cbm], [R_])
            tt("dve", v3(xw.t[:], 8), v3(xB3[:, i, 0:512], 8), ea_out.t[:, i * 32 + 24:i * 32 + 32].unsqueeze(2).to_broadcast([128, 8, 64]), ALU.mult,
               [xB[i], ea_out], [xw])

        def ssd_back(i):
            tsl = slice(i * 128, (i + 1) * 128)
            R_ = Rb2[i % 2]
            eo0 = i * 32
            fns = []
            for h in range(8):
                fns.append(mm(M[3].t[:, h * 64:(h + 1) * 64], R_.t[:, h * 128:(h + 1) * 128], xB3[:, i, h * 64:(h + 1) * 64], True, False))
                fns.append(mm(M[3].t[:, h * 64:(h + 1) * 64], dsk3[:, h, :], xB3[:, i, h * 64:(h + 1) * 64], False, True))
            pe(fns, [R_, xB[i], dsk], [M[3]])
            yo = M[0]
            pe([mm(yo.t[:, gg * 256:(gg + 1) * 256], xact3[:, 6 + gg, tsl], stbf.t[:, gg * 256:(gg + 1) * 256], True, True) for gg in range(2)],
               xact[6:8] + [stbf], [yo])
            sp_ = M[1]
            pe([mm(sp_.t[:, gg * 256:(gg + 1) * 256], xB3[:, i, 512 + gg * 128:512 + (gg + 1) * 128], xw.t[:, gg * 256:(gg + 1) * 256], True, True)
                for gg in range(2)], [xB[i], xw], [sp_])
            tt("dve", v3(state.t[:], 8), v3(state.t[:], 8), ea_out.t[:, eo0 + 8:eo0 + 16].unsqueeze(2).to_broadcast([128, 8, 64]), ALU.mult, [state, ea_out], [state])
            tt("dve", state.t[:], state.t[:], sp_.t[:], ALU.add, [state, sp_], [state])
            cp("act", stbf.t[:], state.t[:], [state], [stbf])
            tt("dve", v3(t1.t[:], 8), v3(yo.t[:], 8), ea_out.t[:, eo0:eo0 + 8].unsqueeze(2).to_broadcast([128, 8, 64]), ALU.mult, [yo, ea_out], [t1])
            tt("dve", t1.t[:], M[3].t[:], t1.t[:], ALU.add, [M[3], t1], [t1])
            tt("dve", t1.t[:], t1.t[:], mixt3[:, i, 512:1024], ALU.mult, [t1, mixt[i]], [t1])
            sq_ = ssq[i % 2]
            for gg in range(2):
                act(junk.t[:, 0:256], t1.t[:, gg * 256:(gg + 1) * 256], AF.Square, [t1], [junk, sq_], accum=sq_.t[:, gg:gg + 1])
            rsqrt_small(rg[i % 2].t[:, 0:2], sq_.t[:, 0:2], sq_.t[:, 2:4], 1.0 / 256, 2, sq_, sq_, rg[i % 2])
            for gg in range(2):
                stt(mixt3[:, i, 512 + gg * 256:512 + (gg + 1) * 256], t1.t[:, gg * 256:(gg + 1) * 256], rg[i % 2].t[:, gg:gg + 1],
                    rowb.t[:, R_SSMW + gg * 256:R_SSMW + (gg + 1) * 256], ALU.mult, ALU.mult, [t1, rg[i % 2], rowb], [mixt[i]])

        ssd_front(0)
        yield W
        for i in range(4):
            pm = mmbank(4)
            fns = []
            for h in range(8):
                fns.append(mm(pm.t[:, h * 64:(h + 1) * 64], wsT3[:, h, :], vl[i].t[:, h * 64:(h + 1) * 64], True, False))
                fns.append(mm(pm.t[:, h * 64:(h + 1) * 64], bsb_t[0:1, h * 128:(h + 1) * 128], onesr_t[0:1, 0:64], False, True))
            pe(fns, [vl[i], wsT, bsb, onesr], [pm])
            tt("dve", mixt3[:, i, 0:512], pm.t[:], mixt3[:, i, 0:512], ALU.mult, [pm, mixt[i]], [mixt[i]])
            tb = tbank()
            pe([tr(tb.t[:, c * 128:(c + 1) * 128], xact3[:, c, i * 128:(i + 1) * 128]) for c in range(6)],
               xact[0:6] + [ident], [tb])
            cp("act", xB3[:, i, :], tb.t[:, 0:768], [tb], [xB[i]])
            yield W
        for i in range(4):
            ssd_mid(i)
            yield W
            if i + 1 < 4:
                ssd_front(i + 1)
                yield W
            ssd_back(i)
            yield W
        ck(6, mixt3[:, 0, 512:1024], mixt)

        s_o0 = load_slot(5)
        s_o1 = load_slot(6)
        so3 = [v3(s_o0.t[:], 8), v3(s_o1.t[:], 8)]
        so_t = [s_o0, s_o1]

        def f_projT(i):
            tb = tbank()
            pe([tr(tb.t[:, k * 128:(k + 1) * 128], mixt3[:, i, k * 128:(k + 1) * 128]) for k in range(8)], [mixt[i], ident], [tb])
            cp("act", hT3[:, :, i * 128:(i + 1) * 128], v3(tb.t[:], 8), [tb], [hT[i]])

        def f_projM(i):
            ob = [M[0], M[1]] if i % 2 == 0 else [M[2], M[3]]
            so_ = ssO[i]
            for n in range(2):
                pe([mm(ob[n].t[:], hT3[:, k, i * 128:(i + 1) * 128], so3[n][:, k, :], k == 0, k == 7) for k in range(8)],
                   [hT[i], so_t[n]], [ob[n]])
                act(junk.t[:, 0:512], ob[n].t[:], AF.Square, [ob[n]], [junk, so_], accum=so_.t[:, n:n + 1])

        def f_post(i):
            ob = [M[0], M[1]] if i % 2 == 0 else [M[2], M[3]]
            so_ = ssO[i]
            tt("dve", so_.t[:, 2:3], so_.t[:, 0:1], so_.t[:, 1:2], ALU.add, [so_], [so_])
            rsqrt_small(rsO[i].t[:, 0:1], so_.t[:, 2:3], so_.t[:, 3:4], 1.0 / 1024, 1, so_, so_, rsO[i])
            for n in range(2):
                t_ = tt_[n]
                stt(t_.t[:], ob[n].t[:], rsO[i].t[:, 0:1], rowb.t[:, R_POST + n * 512:R_POST + (n + 1) * 512], ALU.mult, ALU.mult,
                    [ob[n], rsO[i], rowb], [t_])
                tt("dve", xs[i].t[:, n * 512:(n + 1) * 512], xs[i].t[:, n * 512:(n + 1) * 512], t_.t[:], ALU.add, [xs[i], t_], [xs[i]])
            r0 = tok0 + i * 128
            dma("sp", out_d[r0:r0 + 128, :], xs[i].t[:], [xs[i]], [outT[i]])
            act(junk.t[:], xs[i].t[:], AF.Square, [xs[i]], [junk, ssF[i]], accum=ssF[i].t[:, 0:1])
            rsqrt_small(rsF[i].t[:, 0:1], ssF[i].t[:, 0:1], ssF[i].t[:, 1:2], 1.0 / 1024, 1, ssF[i], ssF[i], rsF[i])
            ts("dve", hb[i % 2].t[:], xs[i].t[:], rsF[i].t[:, 0:1], ALU.mult, [xs[i], rsF[i]], [hb[i % 2]])

        def f_tr(i):
            h_ = hb[i % 2]
            tb = tbank()
            pe([tr(tb.t[:, k * 128:(k + 1) * 128], h_.t[:, k * 128:(k + 1) * 128]) for k in range(8)], [h_, ident], [tb])
            tt("dve", h2T3[:, :, i * 128:(i + 1) * 128], v3(tb.t[:], 8), wpre_b(C_WFFN), ALU.mult, [tb, colp], [h2T[i]])

        f_projT(0)
        f_projT(1)
        yield W
        f_projM(0)
        yield W
        for i in range(4):
            if i + 2 < 4:
                f_projT(i + 2)
            if i + 1 < 4:
                f_projM(i + 1)
                if i + 1 == 3 and g + 1 < NG and stop == 0:
                    preload[(g + 1, 1)] = load_slot(1)
                    preload[(g + 1, 2)] = load_slot(2)
                yield W
            f_post(i)
            yield W
            if i >= 1:
                if i == 1:
                    yield "drain"
                f_tr(i - 1)
                yield W
        f_tr(3)
        if g + 1 < NG and stop == 0:
            prodA(0, tok0 + 512)
            prodA(1, tok0 + 512)
            prodA1(2, tok0 + 512)
            prodA1(3, tok0 + 512)
            pre_issued[g + 1] = True
        yield W
        ck(7, h2T3[:, 0, :], h2T)
        ffn_args[g] = outT

    ffn_args = {}
    frr = {"k": 0}

    def fbank():
        b = frr["k"] % 2
        frr["k"] += 1
        return Fb[b]

    def ffn(g):
        tok0 = g * 512
        outT = ffn_args[g]
        for j in range(8):
            slot = load_slot(7 + j, "f")
            sl3 = v3(slot.t[:], 8)
            for cc in range(4):
                f = j * 4 + cc
                ps_ = fbank()
                pe([mm(ps_.t[:], sl3[:, k, cc * 128:(cc + 1) * 128], h2T3[:, k, :], k == 0, k == 7) for k in range(8)],
                   h2T + [slot], [ps_])
                r_ = rl[f % 2]
                act(r_.t[:], ps_.t[:], AF.Relu, [ps_], [r_])
                tt("dve", fT3[:, f, :], r_.t[:], r_.t[:], ALU.mult, [r_], [fT[f]])
                yield
        ck(8, fT3[:, 0, :], fT)
        o2T3 = h2T3
        for dp in range(4):
            for qh in range(2):
                slot = load_slot(DOWN_BASE + dp * 2 + qh, "f")
                sl3 = v3(slot.t[:], 16)
                for db in range(2):
                    bank = Fb[db]
                    pe([mm(bank.t[:], sl3[:, kk, db * 128:(db + 1) * 128], fT3[:, qh * 16 + kk, :],
                           (qh == 0 and kk == 0), (qh == 1 and kk == 15)) for kk in range(16)],
                       fT[qh * 16:(qh + 1) * 16] + [slot], [bank])
                    if qh == 1:
                        act(o2T3[:, dp * 2 + db, :], bank.t[:], AF.Copy, [bank], h2T)
                    yield
        for i in range(4):
            tb = tbank()
            pe([tr(tb.t[:, k * 128:(k + 1) * 128], o2T3[:, k, i * 128:(i + 1) * 128]) for k in range(8)], [h2T[i], ident], [tb])
            sd_ = ssD[i]
            cp("dve", o2k.t[:], tb.t[:], [tb], [o2k])
            act(junk.t[:], o2k.t[:], AF.Square, [o2k], [junk, sd_], accum=sd_.t[:, 0:1])
            rsqrt_small(rsD[i].t[:, 0:1], sd_.t[:, 0:1], sd_.t[:, 1:2], 1.0 / 1024, 1, sd_, sd_, rsD[i])
            r0 = tok0 + i * 128
            for nn in range(2):
                t_ = ttF[nn]
                stt(t_.t[:], o2k.t[:, nn * 512:(nn + 1) * 512], rsD[i].t[:, 0:1], rowb.t[:, R_POST2 + nn * 512:R_POST2 + (nn + 1) * 512],
                    ALU.mult, ALU.mult, [o2k, rsD[i], rowb], [t_])
                dma("pool", out_d[r0:r0 + 128, nn * 512:(nn + 1) * 512], t_.t[:], [t_, outT[i]], [outT[i]], accum=True)
            yield

    def run_pair(a, b):
        budget = 0.0
        a_live, b_live = a is not None, b is not None
        while a_live or b_live:
            if a_live:
                try:
                    r = next(a)
                except StopIteration:
                    a_live = False
                    r = None
                convert_some(1)
                if r == "drain":
                    while b_live:
                        try:
                            next(b)
                        except StopIteration:
                            b_live = False
                elif isinstance(r, float):
                    budget += r
            else:
                budget += 1.0
            while b_live and budget >= 1.0:
                budget -= 1.0
                try:
                    next(b)
                except StopIteration:
                    b_live = False
            if not b_live:
                budget = 0.0

    try:
        prev = None
        for g in range(NG):
            run_pair(mixer(g), prev)
            prev = ffn(g)
        run_pair(None, prev)
    except _Stop:
        pass
    P.finish()
    P.emit()
    return nc


def _pack_weights(w_in, w_out, w_up, w_down):
    def slot(w, c0):
        blk = w[:, c0:c0 + 512].reshape(8, 128, 512).transpose(1, 0, 2)
        return np.ascontiguousarray(blk).reshape(128, 4096)

    sl = []
    for s in range(5):
        sl.append(slot(w_in, s * 512))
    for n in range(2):
        sl.append(slot(w_out, n * 512))
    for j in range(8):
        sl.append(slot(w_up, j * 512))
    for dp in range(4):
        for qh in range(2):
            blk = w_down[qh * 2048:(qh + 1) * 2048, dp * 256:(dp + 1) * 256].reshape(16, 128, 256).transpose(1, 0, 2)
            sl.append(np.ascontiguousarray(blk).reshape(128, 4096))
    return np.stack(sl, 0).astype(np.float32)


def make_in_maps(inputs, ncores, NG):
    x = np.asarray(inputs["x"], np.float32)
    w_in = np.asarray(inputs["w_in"], np.float32)[0]
    wsl = _pack_weights(w_in, np.asarray(inputs["w_out"], np.float32)[0], np.asarray(inputs["w_up"], np.float32)[0],
                        np.asarray(inputs["w_down"], np.float32)[0])
    wdt = np.ascontiguousarray(w_in[:, 2560:2568].reshape(8, 128, 8).transpose(1, 0, 2)).reshape(128, 64)
    g = lambda k: np.asarray(inputs[k], np.float32)[0]
    rowp = np.concatenate([g("norm_mix_post"), g("norm_ffn_post"), g("gm_ln_w").reshape(-1), g("gm_ln_b").reshape(-1),
                           g("ssm_norm_w"), g("dt_bias"), g("a_log"), g("d_skip")]).astype(np.float32)
    assert rowp.shape[0] == R_END
    colp = np.zeros((128, C_END), np.float32)
    colp[:, C_WPRE:C_WPRE + 8] = g("norm_mix_pre").reshape(8, 128).T
    colp[:, C_WFFN:C_WFFN + 8] = g("norm_ffn_pre").reshape(8, 128).T
    cw = g("conv_w")
    colp[:, C_CW:C_CW + 32] = cw.reshape(4, 8, 128).transpose(2, 1, 0).reshape(128, 32)
    colp[:, C_CB:C_CB + 8] = g("conv_b").reshape(8, 128).T
    ws = g("gm_w_s")
    wst = np.ascontiguousarray(ws.transpose(2, 0, 1)).reshape(128, 1024)
    bs = g("gm_b_s").reshape(1, 1024)
    xf = x.reshape(-1, 1024)
    ntok = NG * 512
    maps = []
    for c in range(ncores):
        maps.append({"x": np.ascontiguousarray(xf[c * ntok:(c + 1) * ntok]), "wslots": wsl, "wdt": wdt.astype(np.float32),
                     "rowpack": rowp, "colpack": colp, "wst": wst.astype(np.float32), "bsrow": bs.astype(np.float32)})
    return maps


def kernel(**inputs):
    NG = 8
    ncores = 8
    nc = build(NG)
    maps = make_in_maps(inputs, ncores, NG)
    res = run_bass_kernel_spmd(nc, maps, core_ids=list(range(ncores)))
    outs = [np.asarray(r["out"], np.float32) for r in res.results]
    return np.concatenate(outs, 0).reshape(16, 2048, 1024)
```
